# Optimizing a Trainium2 kernel written in Bass

```python
import jax, jax.numpy as jnp
from jax import lax
import numpy as np

D_MODEL = 1024
BATCH = 2
SEQ = 8192
DEPTH = 4
DEC_BATCH = 4
DEC_SEQ = 8192
PAST_LEN = 128

GRID_W = 64
D_MIX = D_MODEL
D_POOL = D_MODEL // 4
POOL_GROUPS = 4
POOL_GW = D_POOL // POOL_GROUPS
POOL_WINDOWS = (2, 4, 8, 16)
HEAD_DIM = 64
N_Q_HEADS = (D_MODEL // 2) // HEAD_DIM
N_KV_HEADS = 2
GQA_GROUP = N_Q_HEADS // N_KV_HEADS
D_ATTN = N_Q_HEADS * HEAD_DIM
D_KV = N_KV_HEADS * HEAD_DIM
D_RG = D_MODEL // 4
RG_BLOCKS = 4
RG_BW = D_RG // RG_BLOCKS
CONV_W = 4
RG_C = 8.0
ROPE_THETA = 10000.0
Q_BLOCK = 128
EPS = 1e-6
D_IN = D_POOL + D_ATTN + 2 * D_KV + D_RG + D_MIX

kernel_name = "hymba_pool_gqa_rglru_encoder"


def rmsnorm(x, g):
    xf = x.astype(jnp.float32)
    y = xf * lax.rsqrt(jnp.mean(xf * xf, axis=-1, keepdims=True) + EPS) * g.astype(jnp.float32)
    return y.astype(x.dtype)


def axial_rope_tables(T):
    rows_n = T // GRID_W
    row = jnp.repeat(jnp.arange(rows_n, dtype=jnp.float32), GRID_W)
    col = jnp.tile(jnp.arange(GRID_W, dtype=jnp.float32), rows_n)
    n_f = HEAD_DIM // 4
    inv = ROPE_THETA ** (-jnp.arange(n_f, dtype=jnp.float32) / n_f)
    ang = jnp.concatenate([row[:, None] * inv, col[:, None] * inv], axis=-1)
    return jnp.cos(ang), jnp.sin(ang)


def apply_rope(x, cos, sin):
    xf = x.astype(jnp.float32).reshape(x.shape[:-1] + (HEAD_DIM // 2, 2))
    x1, x2 = xf[..., 0], xf[..., 1]
    c = cos[None, :, None, :]
    s = sin[None, :, None, :]
    out = jnp.stack([x1 * c - x2 * s, x1 * s + x2 * c], axis=-1)
    return out.reshape(x.shape).astype(x.dtype)


def pool_mixer(u, pool_w, pool_scale):
    B, T, _ = u.shape
    uf = u.astype(jnp.float32)
    cs = jnp.concatenate([jnp.zeros((B, 1, D_POOL), jnp.float32), jnp.cumsum(uf, axis=1)], axis=1)
    t = jnp.arange(T)
    outs = []
    for g, w in enumerate(POOL_WINDOWS):
        lo = jnp.clip(t - w // 2, 0, T)
        hi = jnp.clip(t + w // 2, 0, T)
        sl = slice(g * POOL_GW, (g + 1) * POOL_GW)
        csg = cs[..., sl]
        mean = (jnp.take(csg, hi, axis=1) - jnp.take(csg, lo, axis=1)) / (hi - lo).astype(jnp.float32)[None, :, None]
        diff = (mean - uf[..., sl]).astype(u.dtype)
        outs.append(jnp.einsum('btc,cd->btd', diff, pool_w[g]))
    return jnp.concatenate(outs, axis=-1) * pool_scale


def gqa_attention(q, k, v, q_g, k_g, cos, sin):
    B, T, _ = q.shape
    q = q.reshape(B, T, N_Q_HEADS, HEAD_DIM)
    k = k.reshape(B, T, N_KV_HEADS, HEAD_DIM)
    v = v.reshape(B, T, N_KV_HEADS, HEAD_DIM)
    q = apply_rope(rmsnorm(q, q_g), cos, sin)
    k = apply_rope(rmsnorm(k, k_g), cos, sin)
    nb = T // Q_BLOCK
    qb = q.reshape(B, nb, Q_BLOCK, N_KV_HEADS, GQA_GROUP, HEAD_DIM).transpose(1, 0, 2, 3, 4, 5)
    scale = HEAD_DIM ** -0.5

    def block(q_blk):
        s = jnp.einsum('bqkgd,bskd->bkgqs', q_blk, k, preferred_element_type=jnp.float32) * scale
        p = jax.nn.softmax(s, axis=-1)
        return jnp.einsum('bkgqs,bskd->bqkgd', p.astype(v.dtype), v)

    o = lax.map(block, qb)
    return o.transpose(1, 0, 2, 3, 4, 5).reshape(B, T, D_ATTN)


def _lin_comb(e1, e2):
    a1, b1 = e1
    a2, b2 = e2
    return a1 * a2, a2 * b1 + b2


def rglru_mixer(u, conv_w, conv_b, rg_wa, rg_ba, rg_wx, rg_bx, rg_lambda):
    B, T, _ = u.shape
    up = jnp.pad(u, ((0, 0), (CONV_W // 2, CONV_W - 1 - CONV_W // 2), (0, 0)))
    xc = conv_b + up[:, 0:T] * conv_w[0]
    for j in range(1, CONV_W):
        xc = xc + up[:, j:j + T] * conv_w[j]
    xb = xc.reshape(B, T, RG_BLOCKS, RG_BW)
    r = jax.nn.sigmoid(jnp.einsum('btnd,znde->zbtne', xb, rg_wa).reshape(2, B, T, D_RG).astype(jnp.float32)
                       + rg_ba.astype(jnp.float32)[:, None, None, :])
    i = jax.nn.sigmoid(jnp.einsum('btnd,znde->zbtne', xb, rg_wx).reshape(2, B, T, D_RG).astype(jnp.float32)
                       + rg_bx.astype(jnp.float32)[:, None, None, :])
    log_a = -RG_C * r * jax.nn.softplus(-rg_lambda.astype(jnp.float32))[:, None, None, :]
    a = jnp.exp(log_a)
    b = jnp.sqrt(-jnp.expm1(2.0 * log_a)) * (i * xc.astype(jnp.float32)[None])
    _, h_f = lax.associative_scan(_lin_comb, (a[0], b[0]), axis=1)
    _, h_b = lax.associative_scan(_lin_comb, (a[1], b[1]), axis=1, reverse=True)
    return (h_f + h_b).astype(u.dtype)


def hybrid_layer(x, c, cos, sin, ada_w, ada_b, norm_pre, norm_post, w_in, w_out, pool_w, pool_scale,
                 q_norm, k_norm, conv_w, conv_b, rg_wa, rg_ba, rg_wx, rg_bx, rg_lambda):
    mod = jax.nn.silu(c) @ ada_w + ada_b
    shift, scale, gate = jnp.split(mod, 3, axis=-1)
    h = rmsnorm(x, norm_pre) * (1.0 + scale[:, None, :]) + shift[:, None, :]
    u = h @ w_in
    o0 = D_POOL
    o1 = o0 + D_ATTN
    o2 = o1 + D_KV
    o3 = o2 + D_KV
    o4 = o3 + D_RG
    y_pool = pool_mixer(u[..., :o0], pool_w, pool_scale)
    y_attn = gqa_attention(u[..., o0:o1], u[..., o1:o2], u[..., o2:o3], q_norm, k_norm, cos, sin)
    y_rg = rglru_mixer(u[..., o3:o4], conv_w, conv_b, rg_wa, rg_ba, rg_wx, rg_bx, rg_lambda)
    y = jnp.concatenate([y_pool, y_attn, y_rg], axis=-1) * jax.nn.silu(u[..., o4:])
    y = y @ w_out
    return x + gate[:, None, :] * rmsnorm(y, norm_post)


def setup_inputs(seed: int = 0) -> dict:
    key = jax.random.key(seed)
    ks = jax.random.split(key, 24)
    f32 = jnp.float32
    nrm = lambda k, shape, s: jax.random.normal(k, shape, f32) * s
    u_lam = jax.random.uniform(ks[20], (DEPTH, 2, D_RG), f32, 0.9, 0.999)
    s_lam = u_lam ** (1.0 / RG_C)
    return {
        "x_prompt": nrm(ks[0], (BATCH, SEQ, D_MODEL), 1.0),
        "x_sample": nrm(ks[1], (DEC_BATCH, DEC_SEQ, D_MODEL), 1.0),
        "c_prompt": nrm(ks[2], (BATCH, D_MODEL), 1.0),
        "c_sample": nrm(ks[3], (DEC_BATCH, D_MODEL), 1.0),
        "ada_w": nrm(ks[4], (DEPTH, D_MODEL, 3 * D_MODEL), 0.5 * D_MODEL ** -0.5),
        "ada_b": nrm(ks[5], (DEPTH, 3 * D_MODEL), 0.02),
        "norm_pre": 1.0 + nrm(ks[6], (DEPTH, D_MODEL), 0.05),
        "norm_post": 1.0 + nrm(ks[7], (DEPTH, D_MODEL), 0.05),
        "w_in": nrm(ks[8], (DEPTH, D_MODEL, D_IN), D_MODEL ** -0.5),
        "w_out": nrm(ks[9], (DEPTH, D_MIX, D_MODEL), D_MIX ** -0.5),
        "pool_w": nrm(ks[10], (DEPTH, POOL_GROUPS, POOL_GW, POOL_GW), POOL_GW ** -0.5),
        "pool_scale": 1.0 + nrm(ks[11], (DEPTH, D_POOL), 0.1),
        "q_norm": 1.0 + nrm(ks[12], (DEPTH, HEAD_DIM), 0.05),
        "k_norm": 1.0 + nrm(ks[13], (DEPTH, HEAD_DIM), 0.05),
        "conv_w": nrm(ks[14], (DEPTH, CONV_W, D_RG), CONV_W ** -0.5),
        "conv_b": nrm(ks[15], (DEPTH, D_RG), 0.02),
        "rg_wa": nrm(ks[16], (DEPTH, 2, RG_BLOCKS, RG_BW, RG_BW), RG_BW ** -0.5),
        "rg_ba": nrm(ks[17], (DEPTH, 2, D_RG), 0.02),
        "rg_wx": nrm(ks[18], (DEPTH, 2, RG_BLOCKS, RG_BW, RG_BW), RG_BW ** -0.5),
        "rg_bx": nrm(ks[19], (DEPTH, 2, D_RG), 0.02),
        "rg_lambda": jnp.log(s_lam) - jnp.log1p(-s_lam),
    }


def reference(x_prompt, x_sample, c_prompt, c_sample, ada_w, ada_b, norm_pre, norm_post, w_in, w_out,
              pool_w, pool_scale, q_norm, k_norm, conv_w, conv_b, rg_wa, rg_ba, rg_wx, rg_bx, rg_lambda):
    cos_p, sin_p = axial_rope_tables(x_prompt.shape[1])
    cos_s, sin_s = axial_rope_tables(x_sample.shape[1])
    y_prompt = x_prompt
    y_sample = x_sample
    for l in range(DEPTH):
        lw = (ada_w[l], ada_b[l], norm_pre[l], norm_post[l], w_in[l], w_out[l], pool_w[l], pool_scale[l],
              q_norm[l], k_norm[l], conv_w[l], conv_b[l], rg_wa[l], rg_ba[l], rg_wx[l], rg_bx[l], rg_lambda[l])
        y_prompt = hybrid_layer(y_prompt, c_prompt, cos_p, sin_p, *lw)
        y_sample = hybrid_layer(y_sample, c_sample, cos_s, sin_s, *lw)
    return (y_prompt, y_sample)
```

```python
import math
from contextlib import ExitStack

import numpy as np
import concourse.bass as bass
import concourse.mybir as mybir
from concourse.bass_utils import run_bass_kernel_spmd

F32 = mybir.dt.float32
BF16 = mybir.dt.bfloat16
AF = mybir.ActivationFunctionType
ALU = mybir.AluOpType
AX = mybir.AxisListType

D = 1024
D_IN = 2304
HD = 64
EPS = 1e-6
WINS = (2, 4, 8, 16)
NCORES = 8


class Prog:
    ENGS = ("pe", "act", "dve", "pool", "sp")

    def __init__(self, nc):
        self.nc = nc
        self.ops = []
        self.tag = "S"

    def phase(self, tag):
        prog = self

        class _Ctx:
            def __enter__(self_):
                prog.tag = tag

            def __exit__(self_, *a):
                prog.tag = "S"
                return False
        return _Ctx()

    def eng_obj(self, e):
        nc = self.nc
        return {"pe": nc.tensor, "act": nc.scalar, "dve": nc.vector, "pool": nc.gpsimd, "sp": nc.sync}[e]

    def op(self, eng, fn, reads=(), writes=()):
        self.ops.append(dict(eng=eng, fn=fn, reads=tuple(reads), writes=tuple(writes), dma=None, bar=False, tag=self.tag))

    def dma(self, eng, fn, sem, reads=(), writes=()):
        self.ops.append(dict(eng=eng, fn=fn, reads=tuple(reads), writes=tuple(writes), dma=sem, bar=False, tag=self.tag))

    def barrier(self):
        self.ops.append(dict(bar=True))

    def analyze(self):
        ops = self.ops
        last_w = {}
        readers = {}
        last_eng = {}
        last_dma = {}
        bar_deps = []
        passed = {e: True for e in self.ENGS}
        for j, o in enumerate(ops):
            if o["bar"]:
                bar_deps = sorted(set(list(last_eng.values()) + list(last_dma.values())))
                passed = {e: False for e in self.ENGS}
                last_w.clear()
                readers.clear()
                continue
            deps = {}

            def add(i, kind):
                if i is None:
                    return
                p = ops[i]
                same = (p["eng"] == o["eng"]) and p["dma"] is None and o["dma"] is None
                if same and o["eng"] == "pe":
                    return
                deps[i] = True

            if not passed[o["eng"]]:
                passed[o["eng"]] = True
                for i in bar_deps:
                    p = ops[i]
                    if p["dma"] is None and p["eng"] == o["eng"] and o["dma"] is None:
                        continue
                    deps[i] = True
            for k in o["reads"]:
                add(last_w.get(k), "raw")
            for k in o["writes"]:
                add(last_w.get(k), "waw")
                for r in readers.get(k, ()):
                    add(r, "war")
            for k in o["reads"]:
                lst = readers.setdefault(k, [])
                if o["dma"] is None:
                    lst[:] = [r for r in lst if not (ops[r]["dma"] is None and ops[r]["eng"] == o["eng"])]
                lst.append(j)
            for k in o["writes"]:
                last_w[k] = j
                readers[k] = []
            o["deps"] = sorted(deps)
            if o["dma"] is None:
                last_eng[o["eng"]] = j
            else:
                last_dma[o["dma"]] = j
        for o in ops:
            if not o["bar"]:
                o["signal"] = o["dma"] is not None
        for o in ops:
            if not o["bar"]:
                for i in o["deps"]:
                    ops[i]["signal"] = True
        cnt = {}
        for o in ops:
            if o["bar"]:
                continue
            if o["dma"] is not None:
                s = ("dma", o["dma"])
                cnt[s] = cnt.get(s, 0) + 16
                o["ev"] = (s, cnt[s])
            elif o["signal"]:
                s = ("eng", o["eng"])
                cnt[s] = cnt.get(s, 0) + 1
                o["ev"] = (s, cnt[s])
            else:
                o["ev"] = None
        self.final_counts = cnt

    def emit(self, sems):
        know = {e: {} for e in self.ENGS}
        snap = {}
        nwait = 0
        for o in self.ops:
            if o["bar"]:
                continue
            e = o["eng"]
            eo = self.eng_obj(e)
            need = {}
            for i in o["deps"]:
                s, v = self.ops[i]["ev"]
                if v > need.get(s, 0):
                    need[s] = v
            ke = know[e]
            for s, v in sorted(need.items(), key=lambda kv: -kv[1]):
                if ke.get(s, 0) >= v:
                    continue
                eo.wait_ge(sems[s], v)
                nwait += 1
                ke[s] = v
                for s2, v2 in snap[(s, v)].items():
                    if v2 > ke.get(s2, 0):
                        ke[s2] = v2
            ins = o["fn"](eo)
            if o["ev"] is not None:
                s, v = o["ev"]
                ins.then_inc(sems[s], 16 if s[0] == "dma" else 1)
                sn = dict(ke)
                sn[s] = v
                snap[(s, v)] = sn
        return nwait


def dram_ap(t, offset, dims):
    return bass.AP(t.tensor, t.offset + offset, [list(d) for d in dims])


def build_program(T, L, debug=False, phases="WXYZAEFGHKMIJBCD"):
    NT = T // 128
    NB = T // 512
    TC = min(1024, T)
    NCH = T // TC
    NSUB = TC // 512
    KC = T // 128
    NPAIR = KC // 2

    nc = bass.Bass("TRN2", target_bir_lowering=False)
    dk = "ExternalOutput" if debug else "Internal"

    def din(name, shape, dt=F32):
        return nc.dram_tensor(name, list(shape), dt, kind="ExternalInput").ap()

    x_in = din("x", [T, D])
    c_in = din("c", [128, 8])
    ada_w = din("ada_w", [L, D, 3 * D])
    rowv = din("rowv", [L, 5 * D])
    w_in = din("w_in", [L, D, D_IN])
    w_out = din("w_out", [L, D, D])
    pool_w = din("pool_w", [L, 4, 64, 64])
    rg_w = din("rg_w", [L, 2, 2, 4, 64, 64])
    pvec = din("pvec", [128, L, 24])
    qkn = din("qkn", [L, 128])
    cs_tab = din("cs_tab", [T, 64])
    ident_in = din("ident", [128, 128])
    invc_in = din("invc", [128, 2, 16])

    y_out = nc.dram_tensor("y", [T, D], F32, kind="ExternalOutput").ap()
    xs = nc.dram_tensor("xs", [T, D], F32, kind="Internal").ap()
    QTd = nc.dram_tensor("QTd", [128, 4, T], BF16, kind=dk).ap()
    Gd = nc.dram_tensor("Gd", [8, 128, T], BF16, kind=dk).ap()
    YGd = nc.dram_tensor("YGd", [8, 128, T], BF16, kind=dk).ap()
    UPd = nc.dram_tensor("UPd", [2, 128, T], F32, kind=dk).ap()
    URd = nc.dram_tensor("URd", [2, 128, T], F32, kind=dk).ap()
    HFd = nc.dram_tensor("HFd", [2, 128, T], F32, kind=dk).ap()
    if debug:
        KTd = nc.dram_tensor("KTd", [128, T], BF16, kind=dk).ap()
        Vd = nc.dram_tensor("Vd", [128, NT * 2 * 66], BF16, kind=dk).ap()
        MODd = nc.dram_tensor("MODd", [128, 3 * D], F32, kind=dk).ap()
        DBG1 = nc.dram_tensor("DBG1", [128, D], BF16, kind=dk).ap()
        DBG2 = nc.dram_tensor("DBG2", [128, 2], F32, kind=dk).ap()
        DBG3 = nc.dram_tensor("DBG3", [128, 8 * 512], BF16, kind=dk).ap()
        DBG4 = nc.dram_tensor("DBG4", [128, 3 * D], F32, kind=dk).ap()

    P = Prog(nc)
    es = ExitStack()
    with es:
        def sb(name, shape, dt, stack=es):
            return stack.enter_context(nc.sbuf_tensor("sb_" + name, list(shape), dt))

        def ps(name, shape, dt, stack):
            return stack.enter_context(nc.psum_tensor("pp_" + name, list(shape), dt))

        ident_f = sb("ident_f", [128, 128], F32)
        ident_b = sb("ident_b", [128, 128], BF16)
        ones_f = sb("ones_f", [128, 128], F32)
        pv = sb("pv", [128, L, 24], F32)
        nsp = sb("nsp", [128, L, 4], F32)
        nsp2 = sb("nsp2", [128, L, 4], F32)
        pvn = sb("pvn", [128, L, 8], F32)
        gqk = sb("gqk", [128, L, 128], F32)
        c_t = sb("c_t", [128, 8], F32)
        sc = sb("sc", [128, 8], F32)
        SCb = sb("SCb", [128, 8, 128], F32)
        invc = sb("invc", [128, 2, 16], F32)
        KT = sb("KT", [128, T], BF16)
        V = sb("V", [128, NT, 2, 66], BF16)
        SHb = sb("SHb", [128, D], F32)
        G1b = sb("G1b", [128, D], F32)
        G2b = sb("G2b", [128, D], F32)
        Wb = sb("Wb", [128, 8, D_IN], BF16)
        Wob = sb("Wob", [128, 8, D], BF16)
        PWf = sb("PWf", [128, 2, 128], F32)
        PWb = sb("PWb", [128, 2, 128], BF16)
        RGf = sb("RGf", [128, 8, 128], F32)
        RGb = sb("RGb", [128, 8, 128], BF16)
        negM = sb("negM", [128, 1], F32)
        mqk = sb("mqk", [128, 2], F32)
        GQKb = sb("GQKb", [128, 10, 64], F32)
        GScol = sb("GScol", [128, 16], F32)
        mhalf = sb("mhalf", [128, 16], F32)
        phalf = sb("phalf", [128, 16], F32)

        P.dma("sp", lambda e: e.dma_start(out=ident_f[:], in_=ident_in[:, :]), "ld_c1", writes=["ident_f"])
        P.dma("sp", lambda e: e.dma_start(out=pv[:], in_=pvec[:, :, :]), "ld_c2", writes=["pv"])
        P.dma("sp", lambda e: e.dma_start(out=c_t[:], in_=c_in[:, :]), "ld_c3", writes=["c_t"])
        P.dma("sp", lambda e: e.dma_start(out=invc[:], in_=invc_in[:, :, :]), "ld_c4", writes=["invc"])
        P.dma("sp", lambda e: e.dma_start(out=gqk[:].rearrange("p l n -> p (l n)"), in_=dram_ap(qkn, 0, [[0, 128], [1, L * 128]])), "ld_c5",
              writes=["gqk"])
        P.op("dve", lambda e: e.tensor_copy(ident_b[:], ident_f[:]), reads=["ident_f"], writes=["ident_b"])
        P.op("pool", lambda e: e.memset(ones_f[:], 1.0), writes=["ones_f"])
        P.op("pool", lambda e: e.memset(mhalf[:], -0.5), writes=["mhalf"])
        P.op("pool", lambda e: e.memset(phalf[:], 0.5), writes=["phalf"])
        P.op("pool", lambda e: e.memset(V[:], 1.0), writes=["V"])
        P.op("pool", lambda e: e.memset(PWf[:], 0.0), writes=["PWf"])
        P.op("pool", lambda e: e.memset(RGf[:], 0.0), writes=["RGf"])
        P.op("act", lambda e: e.activation(sc[:], c_t[:], AF.Silu), reads=["c_t"], writes=["sc"])
        P.op("dve", lambda e: e.tensor_copy(SCb[:], sc[:].unsqueeze(2).to_broadcast([128, 8, 128])),
             reads=["sc"], writes=["SCb"])
        P.op("act", lambda e: e.activation(nsp[:], pv[:, :, 20:24], AF.Exp, scale=-1.0), reads=["pv"], writes=["nsp"])
        P.op("act", lambda e: e.activation(nsp[:], nsp[:], AF.Ln, bias=1.0), reads=["nsp"], writes=["nsp"])
        P.op("dve", lambda e: e.tensor_scalar(nsp[:], nsp[:], -8.0, None, ALU.mult), reads=["nsp"], writes=["nsp"])
        P.op("dve", lambda e: e.tensor_scalar(nsp2[:], nsp[:], 2.0, None, ALU.mult), reads=["nsp"], writes=["nsp2"])
        P.op("dve", lambda e: e.tensor_scalar(pvn[:], pv[:, :, 12:20], -1.0, None, ALU.mult), reads=["pv"], writes=["pvn"])
        P.barrier()

        def emit_layer(l):
            x_src = x_in if l == 0 else xs
            x_dst = y_out if l == L - 1 else xs

            with ExitStack() as ph, P.phase('W'):
                aw = [sb(f"aw{i}_{l}", [128, 8, 512], F32, ph) for i in range(2)]
                rowc = [sb(f"rowc{i}_{l}", [1, 1024], F32, ph) for i in range(2)]
                nps = sb(f"nps_{l}", [128, 512], F32, ph)
                wst = [sb(f"wst{i}_{l}", [128, 1024], F32, ph) for i in range(4)]
                ps_mod = [ps(f"psmod{i}_{l}", [128, 512], F32, ph) for i in range(2)]
                ps_np = ps(f"psnp_{l}", [128, 512], F32, ph)
                ps_col = ps(f"pscol_{l}", [128, 512], F32, ph)
                if debug:
                    modt = sb(f"modt_{l}", [128, 3 * D], F32, ph)

                for nch in range(6):
                    s = nch % 2
                    cols = slice(nch * 512, (nch + 1) * 512)
                    P.dma("sp", lambda e, s=s, cols=cols: e.dma_start(
                        out=aw[s][:], in_=ada_w[l].rearrange("(k p) n -> p k n", p=128)[:, :, cols]),
                        f"ld_aw{s}", writes=[f"aw{s}"])
                    P.dma("sp", lambda e, s=s, nch=nch: e.dma_start(
                        out=rowc[s][0:1, 0:512], in_=rowv[l:l + 1, nch * 512:(nch + 1) * 512]),
                        f"ld_rca{s}", writes=[f"rowc{s}a"])
                    if nch >= 2:
                        off = 3 * D + (nch - 2) * 512
                        P.dma("sp", lambda e, s=s, off=off: e.dma_start(
                            out=rowc[s][0:1, 512:1024], in_=rowv[l:l + 1, off:off + 512]),
                            f"ld_rcb{s}", writes=[f"rowc{s}b"])
                    for k in range(8):
                        P.op("pe", lambda e, s=s, k=k: e.matmul(ps_mod[s][:], SCb[:, k, :], aw[s][:, k, :],
                                                                 start=(k == 0), stop=False),
                             reads=["SCb", f"aw{s}"], writes=[f"psmod{s}"])
                    P.op("pe", lambda e, s=s: e.matmul(ps_mod[s][:], ones_f[0:1, :], rowc[s][0:1, 0:512],
                                                        start=False, stop=True),
                         reads=["ones_f", f"rowc{s}a"], writes=[f"psmod{s}"])
                    if debug:
                        P.op("dve", lambda e, s=s, cols=cols: e.tensor_copy(modt[:, cols], ps_mod[s][:]),
                             reads=[f"psmod{s}"], writes=["modt"])
                    if nch < 2:
                        P.op("dve", lambda e, s=s, nch=nch: e.tensor_copy(SHb[:, nch * 512:(nch + 1) * 512], ps_mod[s][:]),
                             reads=[f"psmod{s}"], writes=["SHb"])
                    else:
                        P.op("pe", lambda e, s=s: e.matmul(ps_np[:], ones_f[0:1, :], rowc[s][0:1, 512:1024],
                                                            start=True, stop=True),
                             reads=["ones_f", f"rowc{s}b"], writes=["psnp"])
                        P.op("act", lambda e: e.activation(nps[:], ps_np[:], AF.Copy), reads=["psnp"], writes=["nps"])
                        if nch < 4:
                            dst = G1b[:, (nch - 2) * 512:(nch - 1) * 512]
                            P.op("dve", lambda e, s=s, dst=dst: e.scalar_tensor_tensor(
                                dst, ps_mod[s][:], 1.0, nps[:], ALU.add, ALU.mult),
                                reads=[f"psmod{s}", "nps"], writes=["G1b"])
                        else:
                            dst = G2b[:, (nch - 4) * 512:(nch - 3) * 512]
                            P.op("dve", lambda e, s=s, dst=dst: e.tensor_tensor(dst, ps_mod[s][:], nps[:], ALU.mult),
                                 reads=[f"psmod{s}", "nps"], writes=["G2b"])
                if debug and l == 0:
                    P.dma("sp", lambda e: e.dma_start(out=MODd[:, :], in_=modt[:]), "st_dbg", reads=["modt"])

                for k in range(8):
                    P.op("pe", lambda e, k=k: e.matmul(ps_col[:, k:k + 1], G1b[:, k * 128:(k + 1) * 128], ident_f[:, 0:1], start=True, stop=True),
                         reads=["G1b", "ident_f"], writes=["pscol"])
                    P.op("pe", lambda e, k=k: e.matmul(ps_col[:, 8 + k:9 + k], SHb[:, k * 128:(k + 1) * 128], ident_f[:, 0:1], start=True, stop=True),
                         reads=["SHb", "ident_f"], writes=["pscol"])
                P.op("dve", lambda e: e.tensor_copy(GScol[:], ps_col[:, 0:16]), reads=["pscol"], writes=["GScol"])
                P.tag = 'X'
                P.op("dve", lambda e: e.tensor_reduce(mqk[:, 0:1], gqk[:, l, 0:64], AX.X, ALU.max, apply_absolute_value=True),
                     reads=["gqk"], writes=["mqk"])
                P.op("dve", lambda e: e.tensor_reduce(mqk[:, 1:2], gqk[:, l, 64:128], AX.X, ALU.max, apply_absolute_value=True),
                     reads=["gqk", "mqk"], writes=["mqk"])
                P.op("dve", lambda e: e.scalar_tensor_tensor(negM[:], mqk[:, 0:1], -8.0, mqk[:, 1:2], ALU.mult, ALU.mult),
                     reads=["mqk"], writes=["negM"])
                P.op("pool", lambda e: e.tensor_copy(GQKb[:, 0:8, :], gqk[:, l, 0:64].unsqueeze(1).to_broadcast([128, 8, 64])),
                     reads=["gqk"], writes=["GQKb"])
                P.op("pool", lambda e: e.tensor_copy(GQKb[:, 8:10, :], gqk[:, l, 64:128].unsqueeze(1).to_broadcast([128, 2, 64])),
                     reads=["gqk"], writes=["GQKb"])

                P.tag = 'Y'
                wi = 0
                for k in range(8):
                    for part in range(3):
                        s = wi % 4
                        P.dma("sp", lambda e, s=s, k=k, part=part: e.dma_start(
                            out=wst[s][:, 0:768], in_=w_in[l, k * 128:(k + 1) * 128, part * 768:(part + 1) * 768]),
                            f"ld_w{s}", writes=[f"wst{s}"])
                        if wi % 2 == 0:
                            P.op("dve", lambda e, s=s, k=k, part=part: e.tensor_copy(
                                Wb[:, k, part * 768:(part + 1) * 768], wst[s][:, 0:768]),
                                reads=[f"wst{s}"], writes=["Wb"])
                        else:
                            P.op("act", lambda e, s=s, k=k, part=part: e.activation(
                                Wb[:, k, part * 768:(part + 1) * 768], wst[s][:, 0:768], AF.Copy),
                                reads=[f"wst{s}"], writes=["Wb"])
                        wi += 1
                for k in range(8):
                    s = wi % 4
                    P.dma("sp", lambda e, s=s, k=k: e.dma_start(out=wst[s][:], in_=w_out[l, k * 128:(k + 1) * 128, :]),
                          f"ld_w{s}", writes=[f"wst{s}"])
                    if wi % 2 == 0:
                        P.op("dve", lambda e, s=s, k=k: e.tensor_copy(Wob[:, k, :], wst[s][:]),
                             reads=[f"wst{s}"], writes=["Wob"])
                    else:
                        P.op("act", lambda e, s=s, k=k: e.activation(Wob[:, k, :], wst[s][:], AF.Copy),
                             reads=[f"wst{s}"], writes=["Wob"])
                    wi += 1
                P.tag = 'Z'
                for eh in range(2):
                    P.dma("sp", lambda e, eh=eh: e.dma_start(
                        out=PWf[64 * eh:64 * eh + 64, :, 64 * eh:64 * eh + 64],
                        in_=pool_w[l, eh::2].rearrange("g c d -> c g d")), "ld_pwf", writes=["PWf"])
                    for gate in range(2):
                        for z in range(2):
                            i0 = (z * 2 + gate) * 2
                            P.dma("sp", lambda e, eh=eh, gate=gate, z=z, i0=i0: e.dma_start(
                                out=RGf[64 * eh:64 * eh + 64, i0:i0 + 2, 64 * eh:64 * eh + 64],
                                in_=rg_w[l, gate, z, eh::2].rearrange("n d e -> d n e")), "ld_rgf", writes=["RGf"])
                P.op("dve", lambda e: e.tensor_copy(PWb[:], PWf[:]), reads=["PWf"], writes=["PWb"])
                P.op("dve", lambda e: e.tensor_copy(RGb[:], RGf[:]), reads=["RGf"], writes=["RGb"])
            P.barrier()

            with ExitStack() as ph, P.phase('A'):
                xt = [sb(f"xt{i}_{l}", [128, D], F32, ph) for i in range(3)]
                cst = [sb(f"cst{i}_{l}", [128, 64], F32, ph) for i in range(5)]
                junk = sb(f"junk_{l}", [128, D], BF16, ph)
                ssz = sb(f"ssz_{l}", [128, NT], F32, ph)
                rsA = sb(f"rsA_{l}", [128, NT], F32, ph)
                P.op("pool", lambda e: e.memset(ssz[:], 0.0), writes=["ssz"])
                hb = [sb(f"hb{i}_{l}", [128, D], BF16, ph) for i in range(2)]
                hT = [sb(f"hT{i}_{l}", [128, 8, 512], BF16, ph) for i in range(3)]
                qf = [sb(f"qf{i}_{l}", [128, 10, 64], F32, ph) for i in range(2)]
                sqt = sb(f"sqt_{l}", [128, 10, 64], F32, ph)
                ssq = sb(f"ssq_{l}", [128, 10], F32, ph)
                qnn = [sb(f"qn{i}_{l}", [128, 10, 32, 2], F32, ph) for i in range(2)]
                t1 = sb(f"t1_{l}", [128, 10, 32], F32, ph)
                t2 = sb(f"t2_{l}", [128, 10, 32], F32, ph)
                t3 = sb(f"t3_{l}", [128, 10, 32], F32, ph)
                t4 = sb(f"t4_{l}", [128, 10, 32], F32, ph)
                qr = [sb(f"qr{i}_{l}", [128, 10, 32, 2], BF16, ph) for i in range(2)]
                QTs = [sb(f"QTs{i}_{l}", [128, 4, 512], BF16, ph) for i in range(2)]
                uo = [sb(f"uo{i}_{l}", [128, 512], F32, ph) for i in range(3)]
                go = [sb(f"go{i}_{l}", [128, 512], BF16, ph) for i in range(3)]
                ps_t = ps(f"pst_{l}", [128, 8, 128], BF16, ph)
                ps_q = [ps(f"psq{i}_{l}", [128, 512], F32, ph) for i in range(2)]
                ps_kv = [ps(f"pskv{i}_{l}", [128, 512], F32, ph) for i in range(2)]
                ps_tq = ps(f"pstq_{l}", [128, 8, 128], BF16, ph)
                ps_f = [ps(f"psf{i}_{l}", [128, 512], F32, ph) for i in range(2)]
                NPF = 2

                fcols = [slice(0, 128), slice(128, 256), slice(1024, 1152), slice(1152, 1280)] + \
                        [slice(1280 + c * 128, 1408 + c * 128) for c in range(8)]
                fm_state = {"n": 0, "u": 0, "g": 0}

                def feature_major(blk, chunks):
                    hs = blk % 3
                    tsl = slice(blk * 512, (blk + 1) * 512)
                    for nch in chunks:
                        s = fm_state["n"] % NPF
                        fm_state["n"] += 1
                        for k in range(8):
                            P.op("pe", lambda e, s=s, k=k, nch=nch, hs=hs: e.matmul(
                                ps_f[s][:], Wb[:, k, fcols[nch]], hT[hs][:, k, :], start=(k == 0), stop=(k == 7)),
                                reads=["Wb"] + [f"hT{hs}_{j}_{k}" for j in range(4)], writes=[f"psf{s}"])
                        if nch < 4:
                            us = fm_state["u"] % 3
                            fm_state["u"] += 1
                            dst = (UPd if nch < 2 else URd)[nch % 2]
                            P.op("dve", lambda e, s=s, us=us: e.tensor_copy(uo[us][:], ps_f[s][:]),
                                 reads=[f"psf{s}"], writes=[f"uo{us}"])
                            P.dma("sp", lambda e, us=us, dst=dst, tsl=tsl: e.dma_start(out=dst[:, tsl], in_=uo[us][:]),
                                  f"st_uo{us}", reads=[f"uo{us}"])
                        else:
                            gs = fm_state["g"] % 3
                            fm_state["g"] += 1
                            c = nch - 4
                            P.op("act", lambda e, s=s, gs=gs: e.activation(go[gs][:], ps_f[s][:], AF.Silu),
                                 reads=[f"psf{s}"], writes=[f"go{gs}"])
                            P.dma("sp", lambda e, gs=gs, c=c, tsl=tsl: e.dma_start(out=Gd[c][:, tsl], in_=go[gs][:]),
                                  f"st_go{gs}", reads=[f"go{gs}"])

                def load_x(tt):
                    s = tt % 2
                    x3 = tt % 3
                    P.dma("sp", lambda e, x3=x3, tt=tt: e.dma_start(out=xt[x3][:], in_=x_src[tt * 128:(tt + 1) * 128, :]),
                          f"ld_x{x3}", writes=[f"xt{x3}"])
                    c3 = tt % 5
                    P.dma("sp", lambda e, c3=c3, tt=tt: e.dma_start(out=cst[c3][:], in_=cs_tab[tt * 128:(tt + 1) * 128, :]),
                          f"ld_cs{c3}", writes=[f"cst{c3}"])

                def front0(tt):
                    blk, j = divmod(tt, 4)
                    s = tt % 2
                    hs = blk % 3
                    x3 = tt % 3
                    P.op("act", lambda e, x3=x3, tt=tt: e.activation(junk[:], xt[x3][:], AF.Square, accum_out=ssz[:, tt:tt + 1]),
                         reads=[f"xt{x3}", "ssz"], writes=["junk", f"ssqA_{tt}"])
                    P.op("dve", lambda e, tt=tt: e.tensor_scalar(rsA[:, tt:tt + 1], ssz[:, tt:tt + 1], 1.0 / D, EPS, ALU.mult, ALU.add),
                         reads=[f"ssqA_{tt}"], writes=[f"rsA_{tt}"])
                    P.op("pool", lambda e, tt=tt: e.tensor_tensor(rsA[:, tt:tt + 1], rsA[:, tt:tt + 1], mhalf[:, 0:1], ALU.pow),
                         reads=[f"rsA_{tt}"], writes=[f"rsA_{tt}"])
                    P.op("dve", lambda e, s=s, x3=x3, tt=tt: e.tensor_scalar(hb[s][:], xt[x3][:], rsA[:, tt:tt + 1], None, ALU.mult),
                         reads=[f"xt{x3}", f"rsA_{tt}"], writes=[f"hb{s}"])

                def front1(tt):
                    blk, j = divmod(tt, 4)
                    s = tt % 2
                    hs = blk % 3
                    for k in range(8):
                        P.op("pe", lambda e, s=s, k=k: e.transpose(ps_t[:, k, :], hb[s][:, k * 128:(k + 1) * 128], ident_b[:]),
                             reads=[f"hb{s}", "ident_b"], writes=["pst"])
                    for k in range(8):
                        P.op("act", lambda e, hs=hs, j=j, k=k: e.activation(
                            hT[hs][:, k, j * 128:(j + 1) * 128], ps_t[:, k, :], AF.Identity,
                            bias=GScol[:, 8 + k:9 + k], scale=GScol[:, k:k + 1]),
                            reads=["pst", "GScol"], writes=[f"hT{hs}_{j}_{k}"])
                    for k in range(8):
                        P.op("pe", lambda e, s=s, hs=hs, j=j, k=k: e.matmul(
                            ps_q[s][:], hT[hs][:, k, j * 128:(j + 1) * 128], Wb[:, k, 256:768], start=(k == 0), stop=(k == 7)),
                            reads=[f"hT{hs}_{j}_{k}", "Wb"], writes=[f"psq{s}"])
                    for k in range(8):
                        P.op("pe", lambda e, s=s, hs=hs, j=j, k=k: e.matmul(
                            ps_kv[s][:, 0:256], hT[hs][:, k, j * 128:(j + 1) * 128], Wb[:, k, 768:1024], start=(k == 0), stop=(k == 7)),
                            reads=[f"hT{hs}_{j}_{k}", "Wb"], writes=[f"pskv{s}"])

                def backA(tt):
                    blk, j = divmod(tt, 4)
                    s = tt % 2
                    qs = tt % 2
                    P.op("act", lambda e, qs=qs, s=s: e.activation(qf[qs][:, 0:8, :].rearrange("p (g k) d -> p g k d", k=2),
                                                                 ps_q[s][:].rearrange("p (k g d) -> p g k d", k=2, g=4), AF.Copy),
                         reads=[f"psq{s}"], writes=[f"qf{qs}"])
                    P.op("dve", lambda e, qs=qs, s=s: e.tensor_copy(qf[qs][:, 8:10, :], ps_kv[s][:, 0:128].rearrange("p (h d) -> p h d", d=64)),
                         reads=[f"pskv{s}"], writes=[f"qf{qs}"])
                    P.op("dve", lambda e, tt=tt, s=s: e.tensor_copy(V[:, tt, :, 0:64], ps_kv[s][:, 128:256].rearrange("p (h d) -> p h d", d=64)),
                         reads=[f"pskv{s}"], writes=["V"])
                    P.op("act", lambda e, qs=qs: e.activation(sqt[:], qf[qs][:], AF.Square),
                         reads=[f"qf{qs}"], writes=["sqt"])
                    P.op("dve", lambda e: e.tensor_reduce(ssq[:], sqt[:], AX.X, ALU.add), reads=["sqt"], writes=["ssq"])
                    P.op("dve", lambda e: e.tensor_scalar(ssq[:], ssq[:], 1.0 / HD, EPS, ALU.mult, ALU.add),
                         reads=["ssq"], writes=["ssq"])
                    P.op("pool", lambda e: e.tensor_tensor(ssq[:], ssq[:], mhalf[:, 0:10], ALU.pow), reads=["ssq"], writes=["ssq"])
                    qn = qnn[qs]
                    qn3 = qn[:].rearrange("p h i two -> p h (i two)")
                    P.op("dve", lambda e, qs=qs, qn3=qn3: e.tensor_tensor(
                        qn3, qf[qs][:], ssq[:].unsqueeze(2).to_broadcast([128, 10, 64]), ALU.mult),
                        reads=[f"qf{qs}", "ssq"], writes=[f"qn{qs}"])
                    P.op("dve", lambda e, qn3=qn3: e.tensor_tensor(qn3, qn3, GQKb[:], ALU.mult),
                         reads=[f"qn{qs}", "GQKb"], writes=[f"qn{qs}"])

                def backB(tt):
                    blk, j = divmod(tt, 4)
                    qs = tt % 2
                    qn = qnn[qs]
                    c3 = tt % 5
                    cosb = cst[c3][:, 0:32].unsqueeze(1).to_broadcast([128, 10, 32])
                    sinb = cst[c3][:, 32:64].unsqueeze(1).to_broadcast([128, 10, 32])
                    x1 = qn[:, :, :, 0]
                    x2 = qn[:, :, :, 1]
                    P.op("dve", lambda e, x1=x1, cosb=cosb: e.tensor_tensor(t1[:], x1, cosb, ALU.mult),
                         reads=[f"qn{qs}", f"cst{c3}"], writes=["t1"])
                    P.op("pool", lambda e, x2=x2, sinb=sinb: e.tensor_tensor(t2[:], x2, sinb, ALU.mult),
                         reads=[f"qn{qs}", f"cst{c3}"], writes=["t2"])
                    P.op("pool", lambda e, x1=x1, sinb=sinb: e.tensor_tensor(t3[:], x1, sinb, ALU.mult),
                         reads=[f"qn{qs}", f"cst{c3}"], writes=["t3"])
                    P.op("dve", lambda e, x2=x2, cosb=cosb: e.tensor_tensor(t4[:], x2, cosb, ALU.mult),
                         reads=[f"qn{qs}", f"cst{c3}"], writes=["t4"])
                    P.op("dve", lambda e, qs=qs: e.tensor_tensor(qr[qs][:, :, :, 0], t1[:], t2[:], ALU.subtract),
                         reads=["t1", "t2"], writes=[f"qr{qs}"])
                    P.op("pool", lambda e, qs=qs: e.tensor_tensor(qr[qs][:, :, :, 1], t3[:], t4[:], ALU.add),
                         reads=["t3", "t4"], writes=[f"qr{qs}"])
                    qr3 = qr[qs][:].rearrange("p h i two -> p (h i two)")
                    for g in range(5):
                        P.op("pe", lambda e, g=g, qr3=qr3: e.transpose(ps_tq[:, g, :], qr3[:, g * 128:(g + 1) * 128], ident_b[:]),
                             reads=[f"qr{qs}", "ident_b"], writes=["pstq"])
                    Qs = blk % 2
                    P.op("dve", lambda e, Qs=Qs, j=j: e.tensor_copy(QTs[Qs][:, :, j * 128:(j + 1) * 128], ps_tq[:, 0:4, :]),
                         reads=["pstq"], writes=[f"QTs{Qs}"])
                    P.op("dve", lambda e, tt=tt: e.tensor_copy(KT[:, tt * 128:(tt + 1) * 128], ps_tq[:, 4, :]),
                         reads=["pstq"], writes=["KT"])
                    if j == 3:
                        P.dma("sp", lambda e, Qs=Qs, blk=blk: e.dma_start(out=QTd[:, :, blk * 512:(blk + 1) * 512], in_=QTs[Qs][:]),
                              f"st_QT{Qs}", reads=[f"QTs{Qs}"])

                def capture(fn):
                    saved = P.ops
                    P.ops = []
                    fn()
                    out = P.ops
                    P.ops = saved
                    return out

                def interleave(*lists):
                    lists = [x for x in lists if x]
                    out = []
                    if not lists:
                        return out
                    n = max(len(x) for x in lists)
                    pos = [0] * len(lists)
                    for i in range(1, n + 1):
                        for li, x in enumerate(lists):
                            tgt = (i * len(x)) // n
                            while pos[li] < tgt:
                                out.append(x[pos[li]])
                                pos[li] += 1
                    return out

                load_x(0)
                carry = []
                for it in range(-3, NT):
                    tf0, tf1, ta, tb = it + 3, it + 2, it + 1, it
                    if tf0 + 1 < NT:
                        load_x(tf0 + 1)
                    lists = []
                    tail_f, tail_b = [], []
                    if 0 <= tf0 < NT:
                        lists.append(capture(lambda: front0(tf0)))
                    if 0 <= tf1 < NT:
                        tail_f = capture(lambda: front1(tf1))
                    if 0 <= ta < NT:
                        lists.append(capture(lambda: backA(ta)))
                    if 0 <= tb < NT:
                        ops_b = capture(lambda: backB(tb))
                        nb_pre = next(i_ for i_, o_ in enumerate(ops_b) if o_["eng"] == "pe")
                        lists.append(ops_b[:nb_pre])
                        tail_b = ops_b[nb_pre:]
                        blk, j = divmod(tb, 4)
                        if blk > 0:
                            lists.append(capture(lambda: feature_major(blk - 1, range(j * 3, j * 3 + 3))))
                    P.ops += carry
                    P.ops += interleave(*lists)
                    P.ops += tail_f[:16] + tail_b
                    carry = tail_f[16:]
                P.ops += carry
                feature_major(NB - 1, range(12))
                if debug and l == 0:
                    P.dma("sp", lambda e: e.dma_start(out=KTd[:, :], in_=KT[:]), "st_dbg", reads=["KT"])
                    P.dma("sp", lambda e: e.dma_start(out=Vd[:, :], in_=V[:].rearrange("p a b c -> p (a b c)")), "st_dbg", reads=["V"])
            P.barrier()

            with ExitStack() as ph:
                def emit_B():
                    with P.phase('B'):
                        pass
                        up = sb(f"up_{l}", [128, TC + 16], F32, ph)
                        sA = sb(f"sA_{l}", [128, TC + 16], F32, ph)
                        sB = sb(f"sB_{l}", [128, TC + 16], F32, ph)
                        df = sb(f"df_{l}", [128, TC], BF16, ph)
                        bt8 = sb(f"bt8_{l}", [128, 8], F32, ph)
                        gl = sb(f"gl_{l}", [128, TC], BF16, ph)
                        yo = sb(f"yo_{l}", [128, TC], BF16, ph)
                        ur = sb(f"ur_{l}", [128, TC + 4], F32, ph)
                        xc = sb(f"xc_{l}", [128, TC], F32, ph)
                        xcb = sb(f"xcb_{l}", [128, TC], BF16, ph)
                        rt = sb(f"rt_{l}", [128, TC], F32, ph)
                        it = sb(f"it_{l}", [128, TC], F32, ph)
                        at = sb(f"at_{l}", [128, TC], F32, ph)
                        bt = sb(f"bt_{l}", [128, TC], F32, ph)
                        ht = sb(f"ht_{l}", [128, TC], F32, ph)
                        hfl = sb(f"hfl_{l}", [128, TC], F32, ph)
                        carry = sb(f"carry_{l}", [128, 2], F32, ph)
                        ps_x = ps(f"psx_{l}", [128, 512], F32, ph)

                        pcnt = 0
                        for c in range(2):
                            for tc in range(NCH):
                                t0 = tc * TC
                                lo = max(t0 - 8, 0)
                                hi = min(t0 + TC + 8, T)
                                if tc == 0:
                                    P.op("pool", lambda e: e.memset(up[:, 0:8], 0.0), writes=["up"])
                                if tc == NCH - 1:
                                    P.op("pool", lambda e: e.memset(up[:, TC + 8:TC + 16], 0.0), writes=["up"])
                                P.dma("sp", lambda e, c=c, lo=lo, hi=hi, t0=t0: e.dma_start(
                                    out=up[:, lo - (t0 - 8):hi - (t0 - 8)], in_=UPd[c][:, lo:hi]), "ld_up", writes=["up"])
                                P.dma("sp", lambda e, c=c, t0=t0: e.dma_start(out=gl[:], in_=Gd[c][:, t0:t0 + TC]), "ld_gl", writes=["gl"])
                                W = TC + 16
                                P.op("dve", lambda e: e.tensor_tensor(sA[:, 1:W], up[:, 0:W - 1], up[:, 1:W], ALU.add),
                                     reads=["up"], writes=["sA"])
                                if c == 0:
                                    P.op("pool", lambda e: e.tensor_tensor(sB[64:128, 2:W - 1], sA[64:128, 1:W - 2], sA[64:128, 3:W], ALU.add),
                                         reads=["sA"], writes=["sB"])
                                else:
                                    P.op("pool", lambda e: e.tensor_tensor(sB[:, 2:W - 1], sA[:, 1:W - 2], sA[:, 3:W], ALU.add),
                                         reads=["sA"], writes=["sB"])
                                    P.op("dve", lambda e: e.tensor_tensor(sA[:, 4:W - 3], sB[:, 2:W - 5], sB[:, 6:W - 1], ALU.add),
                                         reads=["sB"], writes=["sA"])
                                    P.op("pool", lambda e: e.tensor_tensor(sB[64:128, 8:W - 7], sA[64:128, 4:W - 11], sA[64:128, 12:W - 3], ALU.add),
                                         reads=["sA"], writes=["sB"])
                                halves = [(0, sA, "sA"), (1, sB, "sB")]
                                for eh, S, Sk in halves:
                                    w = WINS[2 * c + eh]
                                    pr = slice(64 * eh, 64 * eh + 64)
                                    P.op("dve", lambda e, S=S, pr=pr, w=w: e.scalar_tensor_tensor(
                                        df[pr, :], S[pr, 8:TC + 8], 1.0 / w, up[pr, 8:TC + 8], ALU.mult, ALU.subtract),
                                        reads=[Sk, "up"], writes=["df"])
                                    if tc == 0:
                                        P.op("dve", lambda e, S=S, pr=pr, c=c: e.tensor_tensor(bt8[pr, :], S[pr, 8:16], invc[pr, c, 0:8], ALU.mult),
                                             reads=[Sk, "invc"], writes=["bt8"])
                                        P.op("dve", lambda e, pr=pr: e.tensor_tensor(df[pr, 0:8], bt8[pr, :], up[pr, 8:16], ALU.subtract),
                                             reads=["bt8", "up", "df"], writes=["df"])
                                    if tc == NCH - 1:
                                        P.op("dve", lambda e, S=S, pr=pr, c=c: e.tensor_tensor(bt8[pr, :], S[pr, TC:TC + 8], invc[pr, c, 8:16], ALU.mult),
                                             reads=[Sk, "invc"], writes=["bt8"])
                                        P.op("dve", lambda e, pr=pr: e.tensor_tensor(df[pr, TC - 8:TC], bt8[pr, :], up[pr, TC:TC + 8], ALU.subtract),
                                             reads=["bt8", "up", "df"], writes=["df"])
                                for sub in range(NSUB):
                                    s = pcnt % 2
                                    pcnt += 1
                                    ssl = slice(sub * 512, (sub + 1) * 512)
                                    P.op("pe", lambda e, c=c, ssl=ssl: e.matmul(ps_x[:], PWb[:, c, :], df[:, ssl], start=True, stop=True),
                                         reads=["PWb", "df"], writes=["psx"])
                                    P.op("dve", lambda e, c=c, ssl=ssl: e.scalar_tensor_tensor(
                                        yo[:, ssl], ps_x[:], pv[:, l, c:c + 1], gl[:, ssl], ALU.mult, ALU.mult),
                                        reads=["psx", "pv", "gl"], writes=["yo"])
                                P.dma("sp", lambda e, c=c, t0=t0: e.dma_start(out=YGd[c][:, t0:t0 + TC], in_=yo[:]), "st_yo", reads=["yo"])

                        rcnt = 0
                        for c in range(2):
                            for z in range(2):
                                order = range(NCH) if z == 0 else range(NCH - 1, -1, -1)
                                for ci, tc in enumerate(order):
                                    t0 = tc * TC
                                    lo = max(t0 - 2, 0)
                                    hi = min(t0 + TC + 1, T)
                                    if tc == 0:
                                        P.op("pool", lambda e: e.memset(ur[:, 0:2], 0.0), writes=["ur"])
                                    if tc == NCH - 1:
                                        P.op("pool", lambda e: e.memset(ur[:, TC + 2:TC + 4], 0.0), writes=["ur"])
                                    P.dma("sp", lambda e, c=c, lo=lo, hi=hi, t0=t0: e.dma_start(
                                        out=ur[:, lo - (t0 - 2):hi - (t0 - 2)], in_=URd[c][:, lo:hi]), "ld_ur", writes=["ur"])
                                    if z == 1:
                                        P.dma("sp", lambda e, c=c, t0=t0: e.dma_start(out=hfl[:], in_=HFd[c][:, t0:t0 + TC]),
                                              "ld_hf", reads=[f"HFd{c}_{tc}"], writes=["hfl"])
                                        P.dma("sp", lambda e, c=c, t0=t0: e.dma_start(out=gl[:], in_=Gd[6 + c][:, t0:t0 + TC]),
                                              "ld_gl", writes=["gl"])
                                    cw = lambda j, c=c: pv[:, l, 2 + c * 4 + j:3 + c * 4 + j]
                                    cb = pv[:, l, 10 + c:11 + c]
                                    P.op("dve", lambda e, cw=cw, cb=cb: e.tensor_scalar(xc[:], ur[:, 0:TC], cw(0), cb, ALU.mult, ALU.add),
                                         reads=["ur", "pv"], writes=["xc"])
                                    for jj in range(1, 4):
                                        P.op("dve", lambda e, cw=cw, jj=jj: e.scalar_tensor_tensor(
                                            xc[:], ur[:, jj:jj + TC], cw(jj), xc[:], ALU.mult, ALU.add),
                                            reads=["ur", "pv", "xc"], writes=["xc"])
                                    P.op("dve", lambda e: e.tensor_copy(xcb[:], xc[:]), reads=["xc"], writes=["xcb"])
                                    nba = pvn[:, l, z * 2 + c:z * 2 + c + 1]
                                    nbx = pvn[:, l, 4 + z * 2 + c:5 + z * 2 + c]
                                    for sub in range(NSUB):
                                        s = rcnt % 2
                                        rcnt += 1
                                        ssl = slice(sub * 512, (sub + 1) * 512)
                                        ia = (z * 2 + 0) * 2 + c
                                        ix = (z * 2 + 1) * 2 + c
                                        P.op("pe", lambda e, ia=ia, ssl=ssl: e.matmul(ps_x[:], RGb[:, ia, :], xcb[:, ssl], start=True, stop=True),
                                             reads=["RGb", "xcb"], writes=["psx"])
                                        P.op("act", lambda e, ssl=ssl, nba=nba: e.activation(rt[:, ssl], ps_x[:], AF.Exp, bias=nba, scale=-1.0),
                                             reads=["psx", "pvn"], writes=["rt"])
                                        P.op("pe", lambda e, ix=ix, ssl=ssl: e.matmul(ps_x[:], RGb[:, ix, :], xcb[:, ssl], start=True, stop=True),
                                             reads=["RGb", "xcb"], writes=["psx"])
                                        P.op("act", lambda e, ssl=ssl, nbx=nbx: e.activation(it[:, ssl], ps_x[:], AF.Exp, bias=nbx, scale=-1.0),
                                             reads=["psx", "pvn"], writes=["it"])
                                    for gt, gk in ((rt, "rt"), (it, "it")):
                                        P.op("dve", lambda e, gt=gt: e.tensor_scalar(gt[:], gt[:], 1.0, None, ALU.add), reads=[gk], writes=[gk])
                                        P.op("dve", lambda e, gt=gt: e.reciprocal(gt[:], gt[:]), reads=[gk], writes=[gk])
                                    nspz = nsp[:, l, z * 2 + c:z * 2 + c + 1]
                                    nspz2 = nsp2[:, l, z * 2 + c:z * 2 + c + 1]
                                    P.op("act", lambda e, nspz=nspz: e.activation(at[:], rt[:], AF.Exp, scale=nspz),
                                         reads=["rt", "nsp"], writes=["at"])
                                    P.op("pool", lambda e: e.tensor_tensor(bt[:], at[:], at[:], ALU.mult),
                                         reads=["at"], writes=["bt"])
                                    P.op("act", lambda e: e.activation(bt[:], bt[:], AF.Ln, bias=1.0, scale=-1.0), reads=["bt"], writes=["bt"])
                                    P.op("act", lambda e: e.activation(bt[:], bt[:], AF.Exp, scale=0.5), reads=["bt"], writes=["bt"])
                                    P.op("pool", lambda e: e.tensor_tensor(it[:], it[:], xc[:], ALU.mult), reads=["it", "xc"], writes=["it"])
                                    P.op("dve", lambda e: e.tensor_tensor(bt[:], bt[:], it[:], ALU.mult), reads=["bt", "it"], writes=["bt"])
                                    if z == 0:
                                        init = 0.0 if ci == 0 else carry[:, 0:1]
                                        P.op("dve", lambda e, init=init: e.tensor_tensor_scan(ht[:], at[:], bt[:], init, ALU.mult, ALU.add),
                                             reads=["at", "bt", "carry"], writes=["ht"])
                                        P.op("dve", lambda e: e.tensor_copy(carry[:, 0:1], ht[:, TC - 1:TC]), reads=["ht"], writes=["carry"])
                                        P.dma("sp", lambda e, c=c, t0=t0: e.dma_start(out=HFd[c][:, t0:t0 + TC], in_=ht[:]),
                                              "st_hf", reads=["ht"], writes=[f"HFd{c}_{tc}"])
                                    else:
                                        init = 0.0 if ci == 0 else carry[:, 1:2]
                                        P.op("dve", lambda e, init=init: e.tensor_tensor_scan(
                                            ht[:, ::-1], at[:, ::-1], bt[:, ::-1], init, ALU.mult, ALU.add),
                                            reads=["at", "bt", "carry"], writes=["ht"])
                                        P.op("dve", lambda e: e.tensor_copy(carry[:, 1:2], ht[:, 0:1]), reads=["ht"], writes=["carry"])
                                        P.op("pool", lambda e: e.tensor_tensor(ht[:], ht[:], hfl[:], ALU.add), reads=["ht", "hfl"], writes=["ht"])
                                        P.op("dve", lambda e: e.tensor_tensor(yo[:], ht[:], gl[:], ALU.mult), reads=["ht", "gl"], writes=["yo"])
                                        P.dma("sp", lambda e, c=c, t0=t0: e.dma_start(out=YGd[6 + c][:, t0:t0 + TC], in_=yo[:]),
                                              "st_yo", reads=["yo"])
                def emit_C():
                    with P.phase('C'):
                        pass
                        qtb = [sb(f"qtb{i}_{l}", [128, 4, 512], BF16, ph) for i in range(2)]
                        pT = [sb(f"pT{i}_{l}", [128, 2, 512], BF16, ph) for i in range(3)]
                        glc = [sb(f"glc{i}_{l}", [64, 512], BF16, ph) for i in range(2)]
                        rd = sb(f"rd_{l}", [128, 512], F32, ph)
                        bcs = sb(f"bcs_{l}", [64, 512], F32, ph)
                        bg = sb(f"bg_{l}", [64, 512], F32, ph)
                        yoc = [sb(f"yoc{i}_{l}", [64, 512], BF16, ph) for i in range(2)]
                        ps_s = [ps(f"pss{i}_{l}", [128, 2, 512], F32, ph) for i in range(2)]
                        ps_o = [ps(f"pso{i}_{l}", [128, 512], F32, ph) for i in range(2)]
                        ps_b = ps(f"psb_{l}", [128, 512], F32, ph)
                        osb = [sb(f"osb{i}_{l}", [128, 512], F32, ph) for i in range(2)]
                        scale = HD ** -0.5
                        pcnt = 0

                        def load_qt(qb):
                            P.dma("sp", lambda e, qb=qb: e.dma_start(out=qtb[qb % 2][:], in_=QTd[:, :, qb * 512:(qb + 1) * 512]),
                                  f"ld_qt{qb % 2}", writes=[f"qtb{qb % 2}"])

                        glc2 = [[sb(f"glcx{i}{b}_{l}", [64, 512], BF16, ph) for b in range(2)] for i in range(2)]
                        rd2 = [sb(f"rdx{b}_{l}", [128, 512], F32, ph) for b in range(2)]

                        def epi1():
                            for b in range(2):
                                P.op("dve", lambda e, b=b: e.tensor_copy(osb[b][0:65, :], ps_o[b][0:65, :]),
                                     reads=[f"pso{b}"], writes=[f"osb{b}"])
                            for b in range(2):
                                P.op("dve", lambda e, b=b: e.reciprocal(rd2[b][64:65, :], osb[b][64:65, :]),
                                     reads=[f"osb{b}"], writes=[f"rdx{b}"])

                        def epi2(qb, g, b, par):
                            h = b * 4 + g
                            cch = 2 + h // 2
                            rows = slice(64 * (h % 2), 64 * (h % 2) + 64)
                            qsl = slice(qb * 512, (qb + 1) * 512)
                            P.op("pe", lambda e, b=b: e.matmul(ps_b[0:64, :], ones_f[64:65, 0:64], rd2[b][64:65, :], start=True, stop=True),
                                 reads=["ones_f", f"rdx{b}"], writes=["psb"])
                            P.op("dve", lambda e: e.tensor_copy(bcs[:], ps_b[0:64, :]), reads=["psb"], writes=["bcs"])
                            P.op("pool", lambda e, b=b, par=par: e.tensor_tensor(bg[:], bcs[:], glc2[par][b][:], ALU.mult),
                                 reads=["bcs", f"glcx{par}{b}"], writes=["bg"])
                            P.op("dve", lambda e, b=b: e.tensor_tensor(yoc[b][:], osb[b][0:64, :], bg[:], ALU.mult),
                                 reads=[f"osb{b}", "bg"], writes=[f"yoc{b}"])
                            P.dma("sp", lambda e, b=b, cch=cch, rows=rows, qsl=qsl: e.dma_start(
                                out=YGd[cch][rows, qsl], in_=yoc[b][:]), f"st_yoc{b}", reads=[f"yoc{b}"])

                        load_qt(0)
                        pending = None
                        npair = 0
                        for qb in range(NB):
                            qsl = slice(qb * 512, (qb + 1) * 512)
                            Qs = qb % 2
                            if qb + 1 < NB:
                                load_qt(qb + 1)
                            for g in range(4):
                                par = npair % 2
                                npair += 1
                                for b in range(2):
                                    h = b * 4 + g
                                    cch = 2 + h // 2
                                    rows = slice(64 * (h % 2), 64 * (h % 2) + 64)
                                    P.dma("sp", lambda e, b=b, par=par, cch=cch, rows=rows, qsl=qsl: e.dma_start(
                                        out=glc2[par][b][:], in_=Gd[cch][rows, qsl]), f"ld_glc{par}{b}", writes=[f"glcx{par}{b}"])

                                def qk(kc, g=g, Qs=Qs):
                                    s = kc % 2
                                    for b in range(2):
                                        pr = slice(64 * b, 64 * b + 64)
                                        P.op("pe", lambda e, s=s, b=b, kc=kc, pr=pr: e.matmul(
                                            ps_s[s][:, b, :], KT[pr, kc * 128:(kc + 1) * 128], qtb[Qs][pr, g, :], start=True, stop=True),
                                            reads=["KT", f"qtb{Qs}"], writes=[f"pss{s}"])

                                qk(0)
                                qk(1)
                                for kc in range(KC):
                                    s = kc % 2
                                    ts = pcnt % 3
                                    pcnt += 1
                                    P.op("act", lambda e, s=s, ts=ts: e.activation(pT[ts][:], ps_s[s][:], AF.Exp, bias=negM[:], scale=scale),
                                         reads=[f"pss{s}", "negM"], writes=[f"pT{ts}"])
                                    if kc + 2 < KC:
                                        qk(kc + 2)
                                    for b in range(2):
                                        P.op("pe", lambda e, ts=ts, b=b, kc=kc: e.matmul(
                                            ps_o[b][0:65, :], V[:, kc, b, 0:65], pT[ts][:, b, :], start=(kc == 0), stop=(kc == KC - 1)),
                                            reads=["V", f"pT{ts}"], writes=[f"pso{b}"])
                                    if pending is not None and kc == min(8, KC - 1):
                                        epi2(*pending, 0, 1 - par)
                                    if pending is not None and kc == min(16, KC - 1):
                                        epi2(*pending, 1, 1 - par)
                                        pending = None
                                epi1()
                                pending = (qb, g)
                        epi2(*pending, 0, (npair - 1) % 2)
                        epi2(*pending, 1, (npair - 1) % 2)
                b_ops = capture(emit_B)
                c_ops = capture(emit_C)
                P.ops += interleave(c_ops, b_ops)
            P.barrier()

            with ExitStack() as ph, P.phase('D'):
                NY, NX, NZ = 3, 4, 3
                ygt = [sb(f"ygt{i}_{l}", [128, 8, 512], BF16, ph) for i in range(NY)]
                xr = [sb(f"xr{i}_{l}", [128, D], F32, ph) for i in range(NX)]
                zt = [sb(f"zt{i}_{l}", [128, D], F32, ph) for i in range(NZ)]
                junk2 = sb(f"junk2_{l}", [128, D], BF16, ph)
                ss2z = sb(f"ss2z_{l}", [128, NT], F32, ph)
                rs2 = sb(f"rs2_{l}", [128, NT], F32, ph)
                P.op("pool", lambda e: e.memset(ss2z[:], 0.0), writes=["ss2z"])
                ps_z = [ps(f"psz{i}_{l}", [128, 2, 512], F32, ph) for i in range(3)]

                def load_yg(blk):
                    ys = blk % NY
                    P.dma("sp", lambda e, ys=ys, blk=blk: e.dma_start(
                        out=ygt[ys][:], in_=YGd[:, :, blk * 512:(blk + 1) * 512].rearrange("c p t -> p c t")),
                        f"ld_yg{ys}", writes=[f"ygt{ys}"])

                def load_xr(tt):
                    xs_ = tt % NX
                    P.dma("sp", lambda e, xs_=xs_, tt=tt: e.dma_start(out=xr[xs_][:], in_=x_src[tt * 128:(tt + 1) * 128, :]),
                          f"ld_xr{xs_}", writes=[f"xr{xs_}"])

                for b0 in range(min(2, NB)):
                    load_yg(b0)
                for t0 in range(min(3, NT)):
                    load_xr(t0)
                for tt in range(NT):
                    blk, j = divmod(tt, 4)
                    s = tt % 3
                    ys = blk % NY
                    xs_ = tt % NX
                    zs = tt % NZ
                    if j == 0 and blk + 2 < NB:
                        load_yg(blk + 2)
                    if tt + 3 < NT:
                        load_xr(tt + 3)
                    for n in range(2):
                        for k in range(8):
                            P.op("pe", lambda e, s=s, ys=ys, j=j, n=n, k=k: e.matmul(
                                ps_z[s][:, n, :], ygt[ys][:, k, j * 128:(j + 1) * 128], Wob[:, k, n * 512:(n + 1) * 512],
                                start=(k == 0), stop=(k == 7)),
                                reads=[f"ygt{ys}", "Wob"], writes=[f"psz{s}"])
                    zf = ps_z[s][:].rearrange("p a b -> p (a b)")
                    P.op("act", lambda e, s=s, zf=zf, tt=tt: e.activation(junk2[:], zf, AF.Square, accum_out=ss2z[:, tt:tt + 1]),
                         reads=[f"psz{s}", "ss2z"], writes=["junk2", f"ssq2_{tt}"])
                    P.op("dve", lambda e, tt=tt: e.tensor_scalar(rs2[:, tt:tt + 1], ss2z[:, tt:tt + 1], 1.0 / D, EPS, ALU.mult, ALU.add),
                         reads=[f"ssq2_{tt}"], writes=[f"rs2_{tt}"])
                    P.op("pool", lambda e, tt=tt: e.tensor_tensor(rs2[:, tt:tt + 1], rs2[:, tt:tt + 1], mhalf[:, 0:1], ALU.pow),
                         reads=[f"rs2_{tt}"], writes=[f"rs2_{tt}"])
                    P.op("dve", lambda e, s=s, zs=zs, zf=zf, tt=tt: e.scalar_tensor_tensor(zt[zs][:], zf, rs2[:, tt:tt + 1], G2b[:], ALU.mult, ALU.mult),
                         reads=[f"psz{s}", f"rs2_{tt}", "G2b"], writes=[f"zt{zs}"])
                    P.op("dve", lambda e, zs=zs, xs_=xs_: e.tensor_tensor(zt[zs][:], zt[zs][:], xr[xs_][:], ALU.add),
                         reads=[f"zt{zs}", f"xr{xs_}"], writes=[f"zt{zs}"])
                    P.dma("pool", lambda e, zs=zs, tt=tt: e.dma_start(out=x_dst[tt * 128:(tt + 1) * 128, :], in_=zt[zs][:]),
                          f"st_z{zs}", reads=[f"zt{zs}"])
            P.barrier()

        for l_ in range(L):
            emit_layer(l_)

        P.ops = [o for o in P.ops if o["bar"] or o["tag"] in ("S" + phases)]
        global _LAST_PROG
        _LAST_PROG = P
        P.analyze()
        sems = {}
        for k in P.final_counts:
            sems[k] = es.enter_context(nc.semaphore("s_" + "_".join(k)))
        P.emit(sems)
        for k, v in P.final_counts.items():
            if k[0] == "dma":
                nc.sync.wait_ge(sems[k], v)
    return nc


def rope_table(T):
    rows_n = T // 64
    row = np.repeat(np.arange(rows_n, dtype=np.float32), 64)
    col = np.tile(np.arange(64, dtype=np.float32), rows_n)
    n_f = HD // 4
    inv = (np.float32(10000.0) ** (-np.arange(n_f, dtype=np.float32) / np.float32(n_f))).astype(np.float32)
    ang = np.concatenate([row[:, None] * inv, col[:, None] * inv], axis=-1).astype(np.float32)
    return np.concatenate([np.cos(ang), np.sin(ang)], axis=-1).astype(np.float32)


def pool_invcnt(T):
    tab = np.zeros((128, 2, 16), np.float32)
    for c in range(2):
        for eh in range(2):
            w = WINS[2 * c + eh]
            for i in range(8):
                for t, col in ((i, i), (T - 8 + i, 8 + i)):
                    lo = max(t - w // 2, 0)
                    hi = min(t + w // 2, T)
                    tab[64 * eh:64 * eh + 64, c, col] = 1.0 / float(hi - lo)
    return tab


def shared_inputs(T, L, ada_w, ada_b, norm_pre, norm_post, w_in, w_out, pool_w, pool_scale, q_norm, k_norm,
                  conv_w, conv_b, rg_wa, rg_ba, rg_wx, rg_bx, rg_lambda):
    f = lambda a: np.ascontiguousarray(np.asarray(a, dtype=np.float32))
    pvec = np.zeros((128, L, 24), np.float32)
    for l in range(L):
        pvec[:, l, 0:2] = f(pool_scale)[l].reshape(2, 128).T
        cw = f(conv_w)[l]
        for c in range(2):
            pvec[:, l, 2 + c * 4:6 + c * 4] = cw[:, c * 128:(c + 1) * 128].T
        pvec[:, l, 10:12] = f(conv_b)[l].reshape(2, 128).T
        for z in range(2):
            pvec[:, l, 12 + z * 2:14 + z * 2] = f(rg_ba)[l, z].reshape(2, 128).T
            pvec[:, l, 16 + z * 2:18 + z * 2] = f(rg_bx)[l, z].reshape(2, 128).T
            pvec[:, l, 20 + z * 2:22 + z * 2] = f(rg_lambda)[l, z].reshape(2, 128).T
    return {
        "ada_w": f(ada_w)[:L],
        "rowv": np.ascontiguousarray(np.concatenate([f(ada_b)[:L], f(norm_pre)[:L], f(norm_post)[:L]], axis=1)),
        "w_in": f(w_in)[:L],
        "w_out": f(w_out)[:L],
        "pool_w": f(pool_w)[:L],
        "rg_w": np.ascontiguousarray(np.stack([f(rg_wa)[:L], f(rg_wx)[:L]], axis=1)),
        "pvec": pvec,
        "qkn": np.ascontiguousarray(np.concatenate([f(q_norm)[:L], f(k_norm)[:L]], axis=1)),
        "cs_tab": rope_table(T),
        "ident": np.eye(128, dtype=np.float32),
        "invc": pool_invcnt(T),
    }


_NC_CACHE = {}


def kernel(x_prompt, x_sample, c_prompt, c_sample, ada_w, ada_b, norm_pre, norm_post, w_in, w_out,
           pool_w, pool_scale, q_norm, k_norm, conv_w, conv_b, rg_wa, rg_ba, rg_wx, rg_bx, rg_lambda):
    x_prompt = np.asarray(x_prompt, np.float32)
    x_sample = np.asarray(x_sample, np.float32)
    c_prompt = np.asarray(c_prompt, np.float32)
    c_sample = np.asarray(c_sample, np.float32)
    L = int(np.asarray(ada_w).shape[0])
    T = int(x_prompt.shape[1])
    assert x_sample.shape[1] == T
    seqs = [(x_prompt[b], c_prompt[b]) for b in range(x_prompt.shape[0])] + \
           [(x_sample[b], c_sample[b]) for b in range(x_sample.shape[0])]
    nseq = len(seqs)
    assert nseq <= NCORES
    shared = shared_inputs(T, L, ada_w, ada_b, norm_pre, norm_post, w_in, w_out, pool_w, pool_scale, q_norm,
                           k_norm, conv_w, conv_b, rg_wa, rg_ba, rg_wx, rg_bx, rg_lambda)
    in_maps = []
    for i in range(NCORES):
        xs_, cs_ = seqs[i % nseq]
        m = dict(shared)
        m["x"] = np.ascontiguousarray(xs_)
        m["c"] = np.ascontiguousarray(cs_.reshape(8, 128).T)
        in_maps.append(m)
    key = (T, L)
    if key not in _NC_CACHE:
        _NC_CACHE[key] = build_program(T, L)
    res = run_bass_kernel_spmd(_NC_CACHE[key], in_maps, core_ids=list(range(NCORES)))
    outs = [np.asarray(res.results[i]["y"], np.float32) for i in range(nseq)]
    nb = x_prompt.shape[0]
    y_prompt = np.stack(outs[:nb], axis=0)
    y_sample = np.stack(outs[nb:], axis=0)
    return (y_prompt, y_sample)
```

```python
import math
from contextlib import ExitStack

import numpy as np
import concourse.bass as bass
import concourse.mybir as mybir
from concourse.bass_utils import run_bass_kernel_spmd

F32 = mybir.dt.float32
BF16 = mybir.dt.bfloat16
AF = mybir.ActivationFunctionType
ALU = mybir.AluOpType
AX = mybir.AxisListType

D = 1024
D_IN = 2304
HD = 64
EPS = 1e-6
WINS = (2, 4, 8, 16)
NCORES = 8


class Prog:
    ENGS = ("pe", "act", "dve", "pool", "sp")

    def __init__(self, nc):
        self.nc = nc
        self.ops = []
        self.tag = "S"

    def phase(self, tag):
        prog = self

        class _Ctx:
            def __enter__(self_):
                prog.tag = tag

            def __exit__(self_, *a):
                prog.tag = "S"
                return False
        return _Ctx()

    def eng_obj(self, e):
        nc = self.nc
        return {"pe": nc.tensor, "act": nc.scalar, "dve": nc.vector, "pool": nc.gpsimd, "sp": nc.sync}[e]

    def op(self, eng, fn, reads=(), writes=()):
        self.ops.append(dict(eng=eng, fn=fn, reads=tuple(reads), writes=tuple(writes), dma=None, bar=False, tag=self.tag))

    def dma(self, eng, fn, sem, reads=(), writes=()):
        self.ops.append(dict(eng=eng, fn=fn, reads=tuple(reads), writes=tuple(writes), dma=sem, bar=False, tag=self.tag))

    def barrier(self):
        self.ops.append(dict(bar=True))

    def analyze(self):
        ops = self.ops
        last_w = {}
        readers = {}
        last_eng = {}
        last_dma = {}
        bar_deps = []
        passed = {e: True for e in self.ENGS}
        for j, o in enumerate(ops):
            if o["bar"]:
                bar_deps = sorted(set(list(last_eng.values()) + list(last_dma.values())))
                passed = {e: False for e in self.ENGS}
                last_w.clear()
                readers.clear()
                continue
            deps = {}

            def add(i, kind):
                if i is None:
                    return
                p = ops[i]
                same = (p["eng"] == o["eng"]) and p["dma"] is None and o["dma"] is None
                if same and o["eng"] == "pe":
                    return
                deps[i] = True

            if not passed[o["eng"]]:
                passed[o["eng"]] = True
                for i in bar_deps:
                    p = ops[i]
                    if p["dma"] is None and p["eng"] == o["eng"] and o["dma"] is None:
                        continue
                    deps[i] = True
            for k in o["reads"]:
                add(last_w.get(k), "raw")
            for k in o["writes"]:
                add(last_w.get(k), "waw")
                for r in readers.get(k, ()):
                    add(r, "war")
            for k in o["reads"]:
                lst = readers.setdefault(k, [])
                if o["dma"] is None:
                    lst[:] = [r for r in lst if not (ops[r]["dma"] is None and ops[r]["eng"] == o["eng"])]
                lst.append(j)
            for k in o["writes"]:
                last_w[k] = j
                readers[k] = []
            o["deps"] = sorted(deps)
            if o["dma"] is None:
                last_eng[o["eng"]] = j
            else:
                last_dma[o["dma"]] = j
        for o in ops:
            if not o["bar"]:
                o["signal"] = o["dma"] is not None
        for o in ops:
            if not o["bar"]:
                for i in o["deps"]:
                    ops[i]["signal"] = True
        cnt = {}
        for o in ops:
            if o["bar"]:
                continue
            if o["dma"] is not None:
                s = ("dma", o["dma"])
                cnt[s] = cnt.get(s, 0) + 16
                o["ev"] = (s, cnt[s])
            elif o["signal"]:
                s = ("eng", o["eng"])
                cnt[s] = cnt.get(s, 0) + 1
                o["ev"] = (s, cnt[s])
            else:
                o["ev"] = None
        self.final_counts = cnt

    def emit(self, sems):
        know = {e: {} for e in self.ENGS}
        snap = {}
        nwait = 0
        for o in self.ops:
            if o["bar"]:
                continue
            e = o["eng"]
            eo = self.eng_obj(e)
            need = {}
            for i in o["deps"]:
                s, v = self.ops[i]["ev"]
                if v > need.get(s, 0):
                    need[s] = v
            ke = know[e]
            for s, v in sorted(need.items(), key=lambda kv: -kv[1]):
                if ke.get(s, 0) >= v:
                    continue
                eo.wait_ge(sems[s], v)
                nwait += 1
                ke[s] = v
                for s2, v2 in snap[(s, v)].items():
                    if v2 > ke.get(s2, 0):
                        ke[s2] = v2
            ins = o["fn"](eo)
            if o["ev"] is not None:
                s, v = o["ev"]
                ins.then_inc(sems[s], 16 if s[0] == "dma" else 1)
                sn = dict(ke)
                sn[s] = v
                snap[(s, v)] = sn
        return nwait


def dram_ap(t, offset, dims):
    return bass.AP(t.tensor, t.offset + offset, [list(d) for d in dims])


def build_program(T, L, debug=False, phases="WXYZAEFGHKMIJBCD"):
    NT = T // 128
    NB = T // 512
    TC = min(1024, T)
    NCH = T // TC
    NSUB = TC // 512
    KC = T // 128
    NPAIR = KC // 2

    nc = bass.Bass("TRN2", target_bir_lowering=False)
    dk = "ExternalOutput" if debug else "Internal"

    def din(name, shape, dt=F32):
        return nc.dram_tensor(name, list(shape), dt, kind="ExternalInput").ap()

    x_in = din("x", [T, D])
    c_in = din("c", [128, 8])
    ada_w = din("ada_w", [L, D, 3 * D])
    rowv = din("rowv", [L, 5 * D])
    w_in = din("w_in", [L, D, D_IN])
    w_out = din("w_out", [L, D, D])
    pool_w = din("pool_w", [L, 4, 64, 64])
    rg_w = din("rg_w", [L, 2, 2, 4, 64, 64])
    pvec = din("pvec", [128, L, 24])
    qkn = din("qkn", [L, 128])
    cs_tab = din("cs_tab", [T, 64])
    ident_in = din("ident", [128, 128])
    invc_in = din("invc", [128, 2, 16])

    y_out = nc.dram_tensor("y", [T, D], F32, kind="ExternalOutput").ap()
    xs = nc.dram_tensor("xs", [T, D], F32, kind="Internal").ap()
    QTd = nc.dram_tensor("QTd", [128, 4, T], BF16, kind=dk).ap()
    Gd = nc.dram_tensor("Gd", [8, 128, T], BF16, kind=dk).ap()
    YGd = nc.dram_tensor("YGd", [8, 128, T], BF16, kind=dk).ap()
    UPd = nc.dram_tensor("UPd", [2, 128, T], F32, kind=dk).ap()
    URd = nc.dram_tensor("URd", [2, 128, T], F32, kind=dk).ap()
    HFd = nc.dram_tensor("HFd", [2, 128, T], F32, kind=dk).ap()
    if debug:
        KTd = nc.dram_tensor("KTd", [128, T], BF16, kind=dk).ap()
        Vd = nc.dram_tensor("Vd", [128, NT * 2 * 66], BF16, kind=dk).ap()
        MODd = nc.dram_tensor("MODd", [128, 3 * D], F32, kind=dk).ap()
        DBG1 = nc.dram_tensor("DBG1", [128, D], BF16, kind=dk).ap()
        DBG2 = nc.dram_tensor("DBG2", [128, 2], F32, kind=dk).ap()
        DBG3 = nc.dram_tensor("DBG3", [128, 8 * 512], BF16, kind=dk).ap()
        DBG4 = nc.dram_tensor("DBG4", [128, 3 * D], F32, kind=dk).ap()

    P = Prog(nc)
    es = ExitStack()
    with es:
        def sb(name, shape, dt, stack=es):
            return stack.enter_context(nc.sbuf_tensor("sb_" + name, list(shape), dt))

        def ps(name, shape, dt, stack):
            return stack.enter_context(nc.psum_tensor("pp_" + name, list(shape), dt))

        ident_f = sb("ident_f", [128, 128], F32)
        ident_b = sb("ident_b", [128, 128], BF16)
        ones_f = sb("ones_f", [128, 128], F32)
        pv = sb("pv", [128, L, 24], F32)
        nsp = sb("nsp", [128, L, 4], F32)
        nsp2 = sb("nsp2", [128, L, 4], F32)
        pvn = sb("pvn", [128, L, 8], F32)
        gqk = sb("gqk", [128, L, 128], F32)
        c_t = sb("c_t", [128, 8], F32)
        sc = sb("sc", [128, 8], F32)
        SCb = sb("SCb", [128, 8, 128], F32)
        invc = sb("invc", [128, 2, 16], F32)
        KT = sb("KT", [128, T], BF16)
        V = sb("V", [128, NT, 2, 66], BF16)
        SHb = sb("SHb", [128, D], F32)
        G1b = sb("G1b", [128, D], F32)
        G2b = sb("G2b", [128, D], F32)
        Wb = sb("Wb", [128, 8, D_IN], BF16)
        Wob = sb("Wob", [128, 8, D], BF16)
        PWf = sb("PWf", [128, 2, 128], F32)
        PWb = sb("PWb", [128, 2, 128], BF16)
        RGf = sb("RGf", [128, 8, 128], F32)
        RGb = sb("RGb", [128, 8, 128], BF16)
        negM = sb("negM", [128, 1], F32)
        mqk = sb("mqk", [128, 2], F32)
        GQKb = sb("GQKb", [128, 10, 64], F32)
        GScol = sb("GScol", [128, 16], F32)
        mhalf = sb("mhalf", [128, 16], F32)
        phalf = sb("phalf", [128, 16], F32)

        P.dma("sp", lambda e: e.dma_start(out=ident_f[:], in_=ident_in[:, :]), "ld_c1", writes=["ident_f"])
        P.dma("sp", lambda e: e.dma_start(out=pv[:], in_=pvec[:, :, :]), "ld_c2", writes=["pv"])
        P.dma("sp", lambda e: e.dma_start(out=c_t[:], in_=c_in[:, :]), "ld_c3", writes=["c_t"])
        P.dma("sp", lambda e: e.dma_start(out=invc[:], in_=invc_in[:, :, :]), "ld_c4", writes=["invc"])
        P.dma("sp", lambda e: e.dma_start(out=gqk[:].rearrange("p l n -> p (l n)"), in_=dram_ap(qkn, 0, [[0, 128], [1, L * 128]])), "ld_c5",
              writes=["gqk"])
        P.op("dve", lambda e: e.tensor_copy(ident_b[:], ident_f[:]), reads=["ident_f"], writes=["ident_b"])
        P.op("pool", lambda e: e.memset(ones_f[:], 1.0), writes=["ones_f"])
        P.op("pool", lambda e: e.memset(mhalf[:], -0.5), writes=["mhalf"])
        P.op("pool", lambda e: e.memset(phalf[:], 0.5), writes=["phalf"])
        P.op("pool", lambda e: e.memset(V[:], 1.0), writes=["V"])
        P.op("pool", lambda e: e.memset(PWf[:], 0.0), writes=["PWf"])
        P.op("pool", lambda e: e.memset(RGf[:], 0.0), writes=["RGf"])
        P.op("act", lambda e: e.activation(sc[:], c_t[:], AF.Silu), reads=["c_t"], writes=["sc"])
        P.op("dve", lambda e: e.tensor_copy(SCb[:], sc[:].unsqueeze(2).to_broadcast([128, 8, 128])),
             reads=["sc"], writes=["SCb"])
        P.op("act", lambda e: e.activation(nsp[:], pv[:, :, 20:24], AF.Exp, scale=-1.0), reads=["pv"], writes=["nsp"])
        P.op("act", lambda e: e.activation(nsp[:], nsp[:], AF.Ln, bias=1.0), reads=["nsp"], writes=["nsp"])
        P.op("dve", lambda e: e.tensor_scalar(nsp[:], nsp[:], -8.0, None, ALU.mult), reads=["nsp"], writes=["nsp"])
        P.op("dve", lambda e: e.tensor_scalar(nsp2[:], nsp[:], 2.0, None, ALU.mult), reads=["nsp"], writes=["nsp2"])
        P.op("dve", lambda e: e.tensor_scalar(pvn[:], pv[:, :, 12:20], -1.0, None, ALU.mult), reads=["pv"], writes=["pvn"])
        P.barrier()

        def emit_layer(l):
            x_src = x_in if l == 0 else xs
            x_dst = y_out if l == L - 1 else xs

            with ExitStack() as ph, P.phase('W'):
                aw = [sb(f"aw{i}_{l}", [128, 8, 512], F32, ph) for i in range(2)]
                rowc = [sb(f"rowc{i}_{l}", [1, 1024], F32, ph) for i in range(2)]
                nps = sb(f"nps_{l}", [128, 512], F32, ph)
                wst = [sb(f"wst{i}_{l}", [128, 1024], F32, ph) for i in range(4)]
                ps_mod = [ps(f"psmod{i}_{l}", [128, 512], F32, ph) for i in range(2)]
                ps_np = ps(f"psnp_{l}", [128, 512], F32, ph)
                ps_col = ps(f"pscol_{l}", [128, 512], F32, ph)
                if debug:
                    modt = sb(f"modt_{l}", [128, 3 * D], F32, ph)

                for nch in range(6):
                    s = nch % 2
                    cols = slice(nch * 512, (nch + 1) * 512)
                    P.dma("sp", lambda e, s=s, cols=cols: e.dma_start(
                        out=aw[s][:], in_=ada_w[l].rearrange("(k p) n -> p k n", p=128)[:, :, cols]),
                        f"ld_aw{s}", writes=[f"aw{s}"])
                    P.dma("sp", lambda e, s=s, nch=nch: e.dma_start(
                        out=rowc[s][0:1, 0:512], in_=rowv[l:l + 1, nch * 512:(nch + 1) * 512]),
                        f"ld_rca{s}", writes=[f"rowc{s}a"])
                    if nch >= 2:
                        off = 3 * D + (nch - 2) * 512
                        P.dma("sp", lambda e, s=s, off=off: e.dma_start(
                            out=rowc[s][0:1, 512:1024], in_=rowv[l:l + 1, off:off + 512]),
                            f"ld_rcb{s}", writes=[f"rowc{s}b"])
                    for k in range(8):
                        P.op("pe", lambda e, s=s, k=k: e.matmul(ps_mod[s][:], SCb[:, k, :], aw[s][:, k, :],
                                                                 start=(k == 0), stop=False),
                             reads=["SCb", f"aw{s}"], writes=[f"psmod{s}"])
                    P.op("pe", lambda e, s=s: e.matmul(ps_mod[s][:], ones_f[0:1, :], rowc[s][0:1, 0:512],
                                                        start=False, stop=True),
                         reads=["ones_f", f"rowc{s}a"], writes=[f"psmod{s}"])
                    if debug:
                        P.op("dve", lambda e, s=s, cols=cols: e.tensor_copy(modt[:, cols], ps_mod[s][:]),
                             reads=[f"psmod{s}"], writes=["modt"])
                    if nch < 2:
                        P.op("dve", lambda e, s=s, nch=nch: e.tensor_copy(SHb[:, nch * 512:(nch + 1) * 512], ps_mod[s][:]),
                             reads=[f"psmod{s}"], writes=["SHb"])
                    else:
                        P.op("pe", lambda e, s=s: e.matmul(ps_np[:], ones_f[0:1, :], rowc[s][0:1, 512:1024],
                                                            start=True, stop=True),
                             reads=["ones_f", f"rowc{s}b"], writes=["psnp"])
                        P.op("act", lambda e: e.activation(nps[:], ps_np[:], AF.Copy), reads=["psnp"], writes=["nps"])
                        if nch < 4:
                            dst = G1b[:, (nch - 2) * 512:(nch - 1) * 512]
                            P.op("dve", lambda e, s=s, dst=dst: e.scalar_tensor_tensor(
                                dst, ps_mod[s][:], 1.0, nps[:], ALU.add, ALU.mult),
                                reads=[f"psmod{s}", "nps"], writes=["G1b"])
                        else:
                            dst = G2b[:, (nch - 4) * 512:(nch - 3) * 512]
                            P.op("dve", lambda e, s=s, dst=dst: e.tensor_tensor(dst, ps_mod[s][:], nps[:], ALU.mult),
                                 reads=[f"psmod{s}", "nps"], writes=["G2b"])
                if debug and l == 0:
                    P.dma("sp", lambda e: e.dma_start(out=MODd[:, :], in_=modt[:]), "st_dbg", reads=["modt"])

                for k in range(8):
                    P.op("pe", lambda e, k=k: e.matmul(ps_col[:, k:k + 1], G1b[:, k * 128:(k + 1) * 128], ident_f[:, 0:1], start=True, stop=True),
                         reads=["G1b", "ident_f"], writes=["pscol"])
                    P.op("pe", lambda e, k=k: e.matmul(ps_col[:, 8 + k:9 + k], SHb[:, k * 128:(k + 1) * 128], ident_f[:, 0:1], start=True, stop=True),
                         reads=["SHb", "ident_f"], writes=["pscol"])
                P.op("dve", lambda e: e.tensor_copy(GScol[:], ps_col[:, 0:16]), reads=["pscol"], writes=["GScol"])
                P.tag = 'X'
                P.op("dve", lambda e: e.tensor_reduce(mqk[:, 0:1], gqk[:, l, 0:64], AX.X, ALU.max, apply_absolute_value=True),
                     reads=["gqk"], writes=["mqk"])
                P.op("dve", lambda e: e.tensor_reduce(mqk[:, 1:2], gqk[:, l, 64:128], AX.X, ALU.max, apply_absolute_value=True),
                     reads=["gqk", "mqk"], writes=["mqk"])
                P.op("dve", lambda e: e.scalar_tensor_tensor(negM[:], mqk[:, 0:1], -8.0, mqk[:, 1:2], ALU.mult, ALU.mult),
                     reads=["mqk"], writes=["negM"])
                P.op("pool", lambda e: e.tensor_copy(GQKb[:, 0:8, :], gqk[:, l, 0:64].unsqueeze(1).to_broadcast([128, 8, 64])),
                     reads=["gqk"], writes=["GQKb"])
                P.op("pool", lambda e: e.tensor_copy(GQKb[:, 8:10, :], gqk[:, l, 64:128].unsqueeze(1).to_broadcast([128, 2, 64])),
                     reads=["gqk"], writes=["GQKb"])

                P.tag = 'Y'
                wi = 0
                for k in range(8):
                    for part in range(3):
                        s = wi % 4
                        P.dma("sp", lambda e, s=s, k=k, part=part: e.dma_start(
                            out=wst[s][:, 0:768], in_=w_in[l, k * 128:(k + 1) * 128, part * 768:(part + 1) * 768]),
                            f"ld_w{s}", writes=[f"wst{s}"])
                        if wi % 2 == 0:
                            P.op("dve", lambda e, s=s, k=k, part=part: e.tensor_copy(
                                Wb[:, k, part * 768:(part + 1) * 768], wst[s][:, 0:768]),
                                reads=[f"wst{s}"], writes=["Wb"])
                        else:
                            P.op("act", lambda e, s=s, k=k, part=part: e.activation(
                                Wb[:, k, part * 768:(part + 1) * 768], wst[s][:, 0:768], AF.Copy),
                                reads=[f"wst{s}"], writes=["Wb"])
                        wi += 1
                for k in range(8):
                    s = wi % 4
                    P.dma("sp", lambda e, s=s, k=k: e.dma_start(out=wst[s][:], in_=w_out[l, k * 128:(k + 1) * 128, :]),
                          f"ld_w{s}", writes=[f"wst{s}"])
                    if wi % 2 == 0:
                        P.op("dve", lambda e, s=s, k=k: e.tensor_copy(Wob[:, k, :], wst[s][:]),
                             reads=[f"wst{s}"], writes=["Wob"])
                    else:
                        P.op("act", lambda e, s=s, k=k: e.activation(Wob[:, k, :], wst[s][:], AF.Copy),
                             reads=[f"wst{s}"], writes=["Wob"])
                    wi += 1
                P.tag = 'Z'
                for eh in range(2):
                    P.dma("sp", lambda e, eh=eh: e.dma_start(
                        out=PWf[64 * eh:64 * eh + 64, :, 64 * eh:64 * eh + 64],
                        in_=pool_w[l, eh::2].rearrange("g c d -> c g d")), "ld_pwf", writes=["PWf"])
                    for gate in range(2):
                        for z in range(2):
                            i0 = (z * 2 + gate) * 2
                            P.dma("sp", lambda e, eh=eh, gate=gate, z=z, i0=i0: e.dma_start(
                                out=RGf[64 * eh:64 * eh + 64, i0:i0 + 2, 64 * eh:64 * eh + 64],
                                in_=rg_w[l, gate, z, eh::2].rearrange("n d e -> d n e")), "ld_rgf", writes=["RGf"])
                P.op("dve", lambda e: e.tensor_copy(PWb[:], PWf[:]), reads=["PWf"], writes=["PWb"])
                P.op("dve", lambda e: e.tensor_copy(RGb[:], RGf[:]), reads=["RGf"], writes=["RGb"])
            P.barrier()

            with ExitStack() as ph, P.phase('A'):
                xt = [sb(f"xt{i}_{l}", [128, D], F32, ph) for i in range(3)]
                cst = [sb(f"cst{i}_{l}", [128, 64], F32, ph) for i in range(5)]
                junk = sb(f"junk_{l}", [128, D], BF16, ph)
                ssz = sb(f"ssz_{l}", [128, NT], F32, ph)
                rsA = sb(f"rsA_{l}", [128, NT], F32, ph)
                P.op("pool", lambda e: e.memset(ssz[:], 0.0), writes=["ssz"])
                hb = [sb(f"hb{i}_{l}", [128, D], BF16, ph) for i in range(2)]
                hT = [sb(f"hT{i}_{l}", [128, 8, 512], BF16, ph) for i in range(3)]
                qf = [sb(f"qf{i}_{l}", [128, 10, 64], F32, ph) for i in range(2)]
                sqt = sb(f"sqt_{l}", [128, 10, 64], F32, ph)
                ssq = sb(f"ssq_{l}", [128, 10], F32, ph)
                qnn = [sb(f"qn{i}_{l}", [128, 10, 32, 2], F32, ph) for i in range(2)]
                t1 = sb(f"t1_{l}", [128, 10, 32], F32, ph)
                t2 = sb(f"t2_{l}", [128, 10, 32], F32, ph)
                t3 = sb(f"t3_{l}", [128, 10, 32], F32, ph)
                t4 = sb(f"t4_{l}", [128, 10, 32], F32, ph)
                qr = [sb(f"qr{i}_{l}", [128, 10, 32, 2], BF16, ph) for i in range(2)]
                QTs = [sb(f"QTs{i}_{l}", [128, 4, 512], BF16, ph) for i in range(2)]
                uo = [sb(f"uo{i}_{l}", [128, 512], F32, ph) for i in range(3)]
                go = [sb(f"go{i}_{l}", [128, 512], BF16, ph) for i in range(3)]
                ps_t = ps(f"pst_{l}", [128, 8, 128], BF16, ph)
                ps_q = [ps(f"psq{i}_{l}", [128, 512], F32, ph) for i in range(2)]
                ps_kv = [ps(f"pskv{i}_{l}", [128, 512], F32, ph) for i in range(2)]
                ps_tq = ps(f"pstq_{l}", [128, 8, 128], BF16, ph)
                ps_f = [ps(f"psf{i}_{l}", [128, 512], F32, ph) for i in range(2)]
                NPF = 2

                fcols = [slice(0, 128), slice(128, 256), slice(1024, 1152), slice(1152, 1280)] + \
                        [slice(1280 + c * 128, 1408 + c * 128) for c in range(8)]
                fm_state = {"n": 0, "u": 0, "g": 0}

                def feature_major(blk, chunks):
                    hs = blk % 3
                    tsl = slice(blk * 512, (blk + 1) * 512)
                    for nch in chunks:
                        s = fm_state["n"] % NPF
                        fm_state["n"] += 1
                        for k in range(8):
                            P.op("pe", lambda e, s=s, k=k, nch=nch, hs=hs: e.matmul(
                                ps_f[s][:], Wb[:, k, fcols[nch]], hT[hs][:, k, :], start=(k == 0), stop=(k == 7)),
                                reads=["Wb"] + [f"hT{hs}_{j}_{k}" for j in range(4)], writes=[f"psf{s}"])
                        if nch < 4:
                            us = fm_state["u"] % 3
                            fm_state["u"] += 1
                            dst = (UPd if nch < 2 else URd)[nch % 2]
                            P.op("dve", lambda e, s=s, us=us: e.tensor_copy(uo[us][:], ps_f[s][:]),
                                 reads=[f"psf{s}"], writes=[f"uo{us}"])
                            P.dma("sp", lambda e, us=us, dst=dst, tsl=tsl: e.dma_start(out=dst[:, tsl], in_=uo[us][:]),
                                  f"st_uo{us}", reads=[f"uo{us}"])
                        else:
                            gs = fm_state["g"] % 3
                            fm_state["g"] += 1
                            c = nch - 4
                            P.op("act", lambda e, s=s, gs=gs: e.activation(go[gs][:], ps_f[s][:], AF.Silu),
                                 reads=[f"psf{s}"], writes=[f"go{gs}"])
                            P.dma("sp", lambda e, gs=gs, c=c, tsl=tsl: e.dma_start(out=Gd[c][:, tsl], in_=go[gs][:]),
                                  f"st_go{gs}", reads=[f"go{gs}"])

                def load_x(tt):
                    s = tt % 2
                    x3 = tt % 3
                    P.dma("sp", lambda e, x3=x3, tt=tt: e.dma_start(out=xt[x3][:], in_=x_src[tt * 128:(tt + 1) * 128, :]),
                          f"ld_x{x3}", writes=[f"xt{x3}"])
                    c3 = tt % 5
                    P.dma("sp", lambda e, c3=c3, tt=tt: e.dma_start(out=cst[c3][:], in_=cs_tab[tt * 128:(tt + 1) * 128, :]),
                          f"ld_cs{c3}", writes=[f"cst{c3}"])

                def front0(tt):
                    blk, j = divmod(tt, 4)
                    s = tt % 2
                    hs = blk % 3
                    x3 = tt % 3
                    P.op("act", lambda e, x3=x3, tt=tt: e.activation(junk[:], xt[x3][:], AF.Square, accum_out=ssz[:, tt:tt + 1]),
                         reads=[f"xt{x3}", "ssz"], writes=["junk", f"ssqA_{tt}"])
                    P.op("dve", lambda e, tt=tt: e.tensor_scalar(rsA[:, tt:tt + 1], ssz[:, tt:tt + 1], 1.0 / D, EPS, ALU.mult, ALU.add),
                         reads=[f"ssqA_{tt}"], writes=[f"rsA_{tt}"])
                    P.op("pool", lambda e, tt=tt: e.tensor_tensor(rsA[:, tt:tt + 1], rsA[:, tt:tt + 1], mhalf[:, 0:1], ALU.pow),
                         reads=[f"rsA_{tt}"], writes=[f"rsA_{tt}"])
                    P.op("dve", lambda e, s=s, x3=x3, tt=tt: e.tensor_scalar(hb[s][:], xt[x3][:], rsA[:, tt:tt + 1], None, ALU.mult),
                         reads=[f"xt{x3}", f"rsA_{tt}"], writes=[f"hb{s}"])

                def front1(tt):
                    blk, j = divmod(tt, 4)
                    s = tt % 2
                    hs = blk % 3
                    for k in range(8):
                        P.op("pe", lambda e, s=s, k=k: e.transpose(ps_t[:, k, :], hb[s][:, k * 128:(k + 1) * 128], ident_b[:]),
                             reads=[f"hb{s}", "ident_b"], writes=["pst"])
                    for k in range(8):
                        P.op("act", lambda e, hs=hs, j=j, k=k: e.activation(
                            hT[hs][:, k, j * 128:(j + 1) * 128], ps_t[:, k, :], AF.Identity,
                            bias=GScol[:, 8 + k:9 + k], scale=GScol[:, k:k + 1]),
                            reads=["pst", "GScol"], writes=[f"hT{hs}_{j}_{k}"])
                    for k in range(8):
                        P.op("pe", lambda e, s=s, hs=hs, j=j, k=k: e.matmul(
                            ps_q[s][:], hT[hs][:, k, j * 128:(j + 1) * 128], Wb[:, k, 256:768], start=(k == 0), stop=(k == 7)),
                            reads=[f"hT{hs}_{j}_{k}", "Wb"], writes=[f"psq{s}"])
                    for k in range(8):
                        P.op("pe", lambda e, s=s, hs=hs, j=j, k=k: e.matmul(
                            ps_kv[s][:, 0:256], hT[hs][:, k, j * 128:(j + 1) * 128], Wb[:, k, 768:1024], start=(k == 0), stop=(k == 7)),
                            reads=[f"hT{hs}_{j}_{k}", "Wb"], writes=[f"pskv{s}"])

                def backA(tt):
                    blk, j = divmod(tt, 4)
                    s = tt % 2
                    qs = tt % 2
                    P.op("act", lambda e, qs=qs, s=s: e.activation(qf[qs][:, 0:8, :].rearrange("p (g k) d -> p g k d", k=2),
                                                                 ps_q[s][:].rearrange("p (k g d) -> p g k d", k=2, g=4), AF.Copy),
                         reads=[f"psq{s}"], writes=[f"qf{qs}"])
                    P.op("dve", lambda e, qs=qs, s=s: e.tensor_copy(qf[qs][:, 8:10, :], ps_kv[s][:, 0:128].rearrange("p (h d) -> p h d", d=64)),
                         reads=[f"pskv{s}"], writes=[f"qf{qs}"])
                    P.op("dve", lambda e, tt=tt, s=s: e.tensor_copy(V[:, tt, :, 0:64], ps_kv[s][:, 128:256].rearrange("p (h d) -> p h d", d=64)),
                         reads=[f"pskv{s}"], writes=["V"])
                    P.op("act", lambda e, qs=qs: e.activation(sqt[:], qf[qs][:], AF.Square),
                         reads=[f"qf{qs}"], writes=["sqt"])
                    P.op("dve", lambda e: e.tensor_reduce(ssq[:], sqt[:], AX.X, ALU.add), reads=["sqt"], writes=["ssq"])
                    P.op("dve", lambda e: e.tensor_scalar(ssq[:], ssq[:], 1.0 / HD, EPS, ALU.mult, ALU.add),
                         reads=["ssq"], writes=["ssq"])
                    P.op("pool", lambda e: e.tensor_tensor(ssq[:], ssq[:], mhalf[:, 0:10], ALU.pow), reads=["ssq"], writes=["ssq"])
                    qn = qnn[qs]
                    qn3 = qn[:].rearrange("p h i two -> p h (i two)")
                    P.op("dve", lambda e, qs=qs, qn3=qn3: e.tensor_tensor(
                        qn3, qf[qs][:], ssq[:].unsqueeze(2).to_broadcast([128, 10, 64]), ALU.mult),
                        reads=[f"qf{qs}", "ssq"], writes=[f"qn{qs}"])
                    P.op("dve", lambda e, qn3=qn3: e.tensor_tensor(qn3, qn3, GQKb[:], ALU.mult),
                         reads=[f"qn{qs}", "GQKb"], writes=[f"qn{qs}"])

                def backB(tt):
                    blk, j = divmod(tt, 4)
                    qs = tt % 2
                    qn = qnn[qs]
                    c3 = tt % 5
                    cosb = cst[c3][:, 0:32].unsqueeze(1).to_broadcast([128, 10, 32])
                    sinb = cst[c3][:, 32:64].unsqueeze(1).to_broadcast([128, 10, 32])
                    x1 = qn[:, :, :, 0]
                    x2 = qn[:, :, :, 1]
                    P.op("dve", lambda e, x1=x1, cosb=cosb: e.tensor_tensor(t1[:], x1, cosb, ALU.mult),
                         reads=[f"qn{qs}", f"cst{c3}"], writes=["t1"])
                    P.op("pool", lambda e, x2=x2, sinb=sinb: e.tensor_tensor(t2[:], x2, sinb, ALU.mult),
                         reads=[f"qn{qs}", f"cst{c3}"], writes=["t2"])
                    P.op("pool", lambda e, x1=x1, sinb=sinb: e.tensor_tensor(t3[:], x1, sinb, ALU.mult),
                         reads=[f"qn{qs}", f"cst{c3}"], writes=["t3"])
                    P.op("dve", lambda e, x2=x2, cosb=cosb: e.tensor_tensor(t4[:], x2, cosb, ALU.mult),
                         reads=[f"qn{qs}", f"cst{c3}"], writes=["t4"])
                    P.op("dve", lambda e, qs=qs: e.tensor_tensor(qr[qs][:, :, :, 0], t1[:], t2[:], ALU.subtract),
                         reads=["t1", "t2"], writes=[f"qr{qs}"])
                    P.op("pool", lambda e, qs=qs: e.tensor_tensor(qr[qs][:, :, :, 1], t3[:], t4[:], ALU.add),
                         reads=["t3", "t4"], writes=[f"qr{qs}"])
                    qr3 = qr[qs][:].rearrange("p h i two -> p (h i two)")
                    for g in range(5):
                        P.op("pe", lambda e, g=g, qr3=qr3: e.transpose(ps_tq[:, g, :], qr3[:, g * 128:(g + 1) * 128], ident_b[:]),
                             reads=[f"qr{qs}", "ident_b"], writes=["pstq"])
                    Qs = blk % 2
                    P.op("dve", lambda e, Qs=Qs, j=j: e.tensor_copy(QTs[Qs][:, :, j * 128:(j + 1) * 128], ps_tq[:, 0:4, :]),
                         reads=["pstq"], writes=[f"QTs{Qs}"])
                    P.op("dve", lambda e, tt=tt: e.tensor_copy(KT[:, tt * 128:(tt + 1) * 128], ps_tq[:, 4, :]),
                         reads=["pstq"], writes=["KT"])
                    if j == 3:
                        P.dma("sp", lambda e, Qs=Qs, blk=blk: e.dma_start(out=QTd[:, :, blk * 512:(blk + 1) * 512], in_=QTs[Qs][:]),
                              f"st_QT{Qs}", reads=[f"QTs{Qs}"])

                def capture(fn):
                    saved = P.ops
                    P.ops = []
                    fn()
                    out = P.ops
                    P.ops = saved
                    return out

                def interleave(*lists):
                    lists = [x for x in lists if x]
                    out = []
                    if not lists:
                        return out
                    n = max(len(x) for x in lists)
                    pos = [0] * len(lists)
                    for i in range(1, n + 1):
                        for li, x in enumerate(lists):
                            tgt = (i * len(x)) // n
                            while pos[li] < tgt:
                                out.append(x[pos[li]])
                                pos[li] += 1
                    return out

                load_x(0)
                carry = []
                for it in range(-3, NT):
                    tf0, tf1, ta, tb = it + 3, it + 2, it + 1, it
                    if tf0 + 1 < NT:
                        load_x(tf0 + 1)
                    lists = []
                    tail_f, tail_b = [], []
                    if 0 <= tf0 < NT:
                        lists.append(capture(lambda: front0(tf0)))
                    if 0 <= tf1 < NT:
                        tail_f = capture(lambda: front1(tf1))
                    if 0 <= ta < NT:
                        lists.append(capture(lambda: backA(ta)))
                    if 0 <= tb < NT:
                        ops_b = capture(lambda: backB(tb))
                        nb_pre = next(i_ for i_, o_ in enumerate(ops_b) if o_["eng"] == "pe")
                        lists.append(ops_b[:nb_pre])
                        tail_b = ops_b[nb_pre:]
                        blk, j = divmod(tb, 4)
                        if blk > 0:
                            lists.append(capture(lambda: feature_major(blk - 1, range(j * 3, j * 3 + 3))))
                    P.ops += carry
                    P.ops += interleave(*lists)
                    P.ops += tail_f[:16] + tail_b
                    carry = tail_f[16:]
                P.ops += carry
                feature_major(NB - 1, range(12))
                if debug and l == 0:
                    P.dma("sp", lambda e: e.dma_start(out=KTd[:, :], in_=KT[:]), "st_dbg", reads=["KT"])
                    P.dma("sp", lambda e: e.dma_start(out=Vd[:, :], in_=V[:].rearrange("p a b c -> p (a b c)")), "st_dbg", reads=["V"])
            P.barrier()

            with ExitStack() as ph:
                def emit_B():
                    with P.phase('B'):
                        pass
                        up = sb(f"up_{l}", [128, TC + 16], F32, ph)
                        sA = sb(f"sA_{l}", [128, TC + 16], F32, ph)
                        sB = sb(f"sB_{l}", [128, TC + 16], F32, ph)
                        df = sb(f"df_{l}", [128, TC], BF16, ph)
                        bt8 = sb(f"bt8_{l}", [128, 8], F32, ph)
                        gl = sb(f"gl_{l}", [128, TC], BF16, ph)
                        yo = sb(f"yo_{l}", [128, TC], BF16, ph)
                        ur = sb(f"ur_{l}", [128, TC + 4], F32, ph)
                        xc = sb(f"xc_{l}", [128, TC], F32, ph)
                        xcb = sb(f"xcb_{l}", [128, TC], BF16, ph)
                        rt = sb(f"rt_{l}", [128, TC], F32, ph)
                        it = sb(f"it_{l}", [128, TC], F32, ph)
                        at = sb(f"at_{l}", [128, TC], F32, ph)
                        bt = sb(f"bt_{l}", [128, TC], F32, ph)
                        ht = sb(f"ht_{l}", [128, TC], F32, ph)
                        hfl = sb(f"hfl_{l}", [128, TC], F32, ph)
                        carry = sb(f"carry_{l}", [128, 2], F32, ph)
                        ps_x = ps(f"psx_{l}", [128, 512], F32, ph)

                        pcnt = 0
                        for c in range(2):
                            for tc in range(NCH):
                                t0 = tc * TC
                                lo = max(t0 - 8, 0)
                                hi = min(t0 + TC + 8, T)
                                if tc == 0:
                                    P.op("pool", lambda e: e.memset(up[:, 0:8], 0.0), writes=["up"])
                                if tc == NCH - 1:
                                    P.op("pool", lambda e: e.memset(up[:, TC + 8:TC + 16], 0.0), writes=["up"])
                                P.dma("sp", lambda e, c=c, lo=lo, hi=hi, t0=t0: e.dma_start(
                                    out=up[:, lo - (t0 - 8):hi - (t0 - 8)], in_=UPd[c][:, lo:hi]), "ld_up", writes=["up"])
                                P.dma("sp", lambda e, c=c, t0=t0: e.dma_start(out=gl[:], in_=Gd[c][:, t0:t0 + TC]), "ld_gl", writes=["gl"])
                                W = TC + 16
                                P.op("dve", lambda e: e.tensor_tensor(sA[:, 1:W], up[:, 0:W - 1], up[:, 1:W], ALU.add),
                                     reads=["up"], writes=["sA"])
                                if c == 0:
                                    P.op("pool", lambda e: e.tensor_tensor(sB[64:128, 2:W - 1], sA[64:128, 1:W - 2], sA[64:128, 3:W], ALU.add),
                                         reads=["sA"], writes=["sB"])
                                else:
                                    P.op("pool", lambda e: e.tensor_tensor(sB[:, 2:W - 1], sA[:, 1:W - 2], sA[:, 3:W], ALU.add),
                                         reads=["sA"], writes=["sB"])
                                    P.op("dve", lambda e: e.tensor_tensor(sA[:, 4:W - 3], sB[:, 2:W - 5], sB[:, 6:W - 1], ALU.add),
                                         reads=["sB"], writes=["sA"])
                                    P.op("pool", lambda e: e.tensor_tensor(sB[64:128, 8:W - 7], sA[64:128, 4:W - 11], sA[64:128, 12:W - 3], ALU.add),
                                         reads=["sA"], writes=["sB"])
                                halves = [(0, sA, "sA"), (1, sB, "sB")]
                                for eh, S, Sk in halves:
                                    w = WINS[2 * c + eh]
                                    pr = slice(64 * eh, 64 * eh + 64)
                                    P.op("dve", lambda e, S=S, pr=pr, w=w: e.scalar_tensor_tensor(
                                        df[pr, :], S[pr, 8:TC + 8], 1.0 / w, up[pr, 8:TC + 8], ALU.mult, ALU.subtract),
                                        reads=[Sk, "up"], writes=["df"])
                                    if tc == 0:
                                        P.op("dve", lambda e, S=S, pr=pr, c=c: e.tensor_tensor(bt8[pr, :], S[pr, 8:16], invc[pr, c, 0:8], ALU.mult),
                                             reads=[Sk, "invc"], writes=["bt8"])
                                        P.op("dve", lambda e, pr=pr: e.tensor_tensor(df[pr, 0:8], bt8[pr, :], up[pr, 8:16], ALU.subtract),
                                             reads=["bt8", "up", "df"], writes=["df"])
                                    if tc == NCH - 1:
                                        P.op("dve", lambda e, S=S, pr=pr, c=c: e.tensor_tensor(bt8[pr, :], S[pr, TC:TC + 8], invc[pr, c, 8:16], ALU.mult),
                                             reads=[Sk, "invc"], writes=["bt8"])
                                        P.op("dve", lambda e, pr=pr: e.tensor_tensor(df[pr, TC - 8:TC], bt8[pr, :], up[pr, TC:TC + 8], ALU.subtract),
                                             reads=["bt8", "up", "df"], writes=["df"])
                                for sub in range(NSUB):
                                    s = pcnt % 2
                                    pcnt += 1
                                    ssl = slice(sub * 512, (sub + 1) * 512)
                                    P.op("pe", lambda e, c=c, ssl=ssl: e.matmul(ps_x[:], PWb[:, c, :], df[:, ssl], start=True, stop=True),
                                         reads=["PWb", "df"], writes=["psx"])
                                    P.op("dve", lambda e, c=c, ssl=ssl: e.scalar_tensor_tensor(
                                        yo[:, ssl], ps_x[:], pv[:, l, c:c + 1], gl[:, ssl], ALU.mult, ALU.mult),
                                        reads=["psx", "pv", "gl"], writes=["yo"])
                                P.dma("sp", lambda e, c=c, t0=t0: e.dma_start(out=YGd[c][:, t0:t0 + TC], in_=yo[:]), "st_yo", reads=["yo"])

                        rcnt = 0
                        for c in range(2):
                            for z in range(2):
                                order = range(NCH) if z == 0 else range(NCH - 1, -1, -1)
                                for ci, tc in enumerate(order):
                                    t0 = tc * TC
                                    lo = max(t0 - 2, 0)
                                    hi = min(t0 + TC + 1, T)
                                    if tc == 0:
                                        P.op("pool", lambda e: e.memset(ur[:, 0:2], 0.0), writes=["ur"])
                                    if tc == NCH - 1:
                                        P.op("pool", lambda e: e.memset(ur[:, TC + 2:TC + 4], 0.0), writes=["ur"])
                                    P.dma("sp", lambda e, c=c, lo=lo, hi=hi, t0=t0: e.dma_start(
                                        out=ur[:, lo - (t0 - 2):hi - (t0 - 2)], in_=URd[c][:, lo:hi]), "ld_ur", writes=["ur"])
                                    if z == 1:
                                        P.dma("sp", lambda e, c=c, t0=t0: e.dma_start(out=hfl[:], in_=HFd[c][:, t0:t0 + TC]),
                                              "ld_hf", reads=[f"HFd{c}_{tc}"], writes=["hfl"])
                                        P.dma("sp", lambda e, c=c, t0=t0: e.dma_start(out=gl[:], in_=Gd[6 + c][:, t0:t0 + TC]),
                                              "ld_gl", writes=["gl"])
                                    cw = lambda j, c=c: pv[:, l, 2 + c * 4 + j:3 + c * 4 + j]
                                    cb = pv[:, l, 10 + c:11 + c]
                                    P.op("dve", lambda e, cw=cw, cb=cb: e.tensor_scalar(xc[:], ur[:, 0:TC], cw(0), cb, ALU.mult, ALU.add),
                                         reads=["ur", "pv"], writes=["xc"])
                                    for jj in range(1, 4):
                                        P.op("dve", lambda e, cw=cw, jj=jj: e.scalar_tensor_tensor(
                                            xc[:], ur[:, jj:jj + TC], cw(jj), xc[:], ALU.mult, ALU.add),
                                            reads=["ur", "pv", "xc"], writes=["xc"])
                                    P.op("dve", lambda e: e.tensor_copy(xcb[:], xc[:]), reads=["xc"], writes=["xcb"])
                                    nba = pvn[:, l, z * 2 + c:z * 2 + c + 1]
                                    nbx = pvn[:, l, 4 + z * 2 + c:5 + z * 2 + c]
                                    for sub in range(NSUB):
                                        s = rcnt % 2
                                        rcnt += 1
                                        ssl = slice(sub * 512, (sub + 1) * 512)
                                        ia = (z * 2 + 0) * 2 + c
                                        ix = (z * 2 + 1) * 2 + c
                                        P.op("pe", lambda e, ia=ia, ssl=ssl: e.matmul(ps_x[:], RGb[:, ia, :], xcb[:, ssl], start=True, stop=True),
                                             reads=["RGb", "xcb"], writes=["psx"])
                                        P.op("act", lambda e, ssl=ssl, nba=nba: e.activation(rt[:, ssl], ps_x[:], AF.Exp, bias=nba, scale=-1.0),
                                             reads=["psx", "pvn"], writes=["rt"])
                                        P.op("pe", lambda e, ix=ix, ssl=ssl: e.matmul(ps_x[:], RGb[:, ix, :], xcb[:, ssl], start=True, stop=True),
                                             reads=["RGb", "xcb"], writes=["psx"])
                                        P.op("act", lambda e, ssl=ssl, nbx=nbx: e.activation(it[:, ssl], ps_x[:], AF.Exp, bias=nbx, scale=-1.0),
                                             reads=["psx", "pvn"], writes=["it"])
                                    for gt, gk in ((rt, "rt"), (it, "it")):
                                        P.op("dve", lambda e, gt=gt: e.tensor_scalar(gt[:], gt[:], 1.0, None, ALU.add), reads=[gk], writes=[gk])
                                        P.op("dve", lambda e, gt=gt: e.reciprocal(gt[:], gt[:]), reads=[gk], writes=[gk])
                                    nspz = nsp[:, l, z * 2 + c:z * 2 + c + 1]
                                    nspz2 = nsp2[:, l, z * 2 + c:z * 2 + c + 1]
                                    P.op("act", lambda e, nspz=nspz: e.activation(at[:], rt[:], AF.Exp, scale=nspz),
                                         reads=["rt", "nsp"], writes=["at"])
                                    P.op("pool", lambda e: e.tensor_tensor(bt[:], at[:], at[:], ALU.mult),
                                         reads=["at"], writes=["bt"])
                                    P.op("act", lambda e: e.activation(bt[:], bt[:], AF.Ln, bias=1.0, scale=-1.0), reads=["bt"], writes=["bt"])
                                    P.op("act", lambda e: e.activation(bt[:], bt[:], AF.Exp, scale=0.5), reads=["bt"], writes=["bt"])
                                    P.op("pool", lambda e: e.tensor_tensor(it[:], it[:], xc[:], ALU.mult), reads=["it", "xc"], writes=["it"])
                                    P.op("dve", lambda e: e.tensor_tensor(bt[:], bt[:], it[:], ALU.mult), reads=["bt", "it"], writes=["bt"])
                                    if z == 0:
                                        init = 0.0 if ci == 0 else carry[:, 0:1]
                                        P.op("dve", lambda e, init=init: e.tensor_tensor_scan(ht[:], at[:], bt[:], init, ALU.mult, ALU.add),
                                             reads=["at", "bt", "carry"], writes=["ht"])
                                        P.op("dve", lambda e: e.tensor_copy(carry[:, 0:1], ht[:, TC - 1:TC]), reads=["ht"], writes=["carry"])
                                        P.dma("sp", lambda e, c=c, t0=t0: e.dma_start(out=HFd[c][:, t0:t0 + TC], in_=ht[:]),
                                              "st_hf", reads=["ht"], writes=[f"HFd{c}_{tc}"])
                                    else:
                                        init = 0.0 if ci == 0 else carry[:, 1:2]
                                        P.op("dve", lambda e, init=init: e.tensor_tensor_scan(
                                            ht[:, ::-1], at[:, ::-1], bt[:, ::-1], init, ALU.mult, ALU.add),
                                            reads=["at", "bt", "carry"], writes=["ht"])
                                        P.op("dve", lambda e: e.tensor_copy(carry[:, 1:2], ht[:, 0:1]), reads=["ht"], writes=["carry"])
                                        P.op("pool", lambda e: e.tensor_tensor(ht[:], ht[:], hfl[:], ALU.add), reads=["ht", "hfl"], writes=["ht"])
                                        P.op("dve", lambda e: e.tensor_tensor(yo[:], ht[:], gl[:], ALU.mult), reads=["ht", "gl"], writes=["yo"])
                                        P.dma("sp", lambda e, c=c, t0=t0: e.dma_start(out=YGd[6 + c][:, t0:t0 + TC], in_=yo[:]),
                                              "st_yo", reads=["yo"])
                def emit_C():
                    with P.phase('C'):
                        pass
                        qtb = [sb(f"qtb{i}_{l}", [128, 4, 512], BF16, ph) for i in range(2)]
                        pT = [sb(f"pT{i}_{l}", [128, 2, 512], BF16, ph) for i in range(3)]
                        glc = [sb(f"glc{i}_{l}", [64, 512], BF16, ph) for i in range(2)]
                        rd = sb(f"rd_{l}", [128, 512], F32, ph)
                        bcs = sb(f"bcs_{l}", [64, 512], F32, ph)
                        bg = sb(f"bg_{l}", [64, 512], F32, ph)
                        yoc = [sb(f"yoc{i}_{l}", [64, 512], BF16, ph) for i in range(2)]
                        ps_s = [ps(f"pss{i}_{l}", [128, 2, 512], F32, ph) for i in range(2)]
                        ps_o = [ps(f"pso{i}_{l}", [128, 512], F32, ph) for i in range(2)]
                        ps_b = ps(f"psb_{l}", [128, 512], F32, ph)
                        osb = [sb(f"osb{i}_{l}", [128, 512], F32, ph) for i in range(2)]
                        scale = HD ** -0.5
                        pcnt = 0

                        def load_qt(qb):
                            P.dma("sp", lambda e, qb=qb: e.dma_start(out=qtb[qb % 2][:], in_=QTd[:, :, qb * 512:(qb + 1) * 512]),
                                  f"ld_qt{qb % 2}", writes=[f"qtb{qb % 2}"])

                        glc2 = [[sb(f"glcx{i}{b}_{l}", [64, 512], BF16, ph) for b in range(2)] for i in range(2)]
                        rd2 = [sb(f"rdx{b}_{l}", [128, 512], F32, ph) for b in range(2)]

                        def epi1():
                            for b in range(2):
                                P.op("dve", lambda e, b=b: e.tensor_copy(osb[b][0:65, :], ps_o[b][0:65, :]),
                                     reads=[f"pso{b}"], writes=[f"osb{b}"])
                            for b in range(2):
                                P.op("dve", lambda e, b=b: e.reciprocal(rd2[b][64:65, :], osb[b][64:65, :]),
                                     reads=[f"osb{b}"], writes=[f"rdx{b}"])

                        def epi2(qb, g, b, par):
                            h = b * 4 + g
                            cch = 2 + h // 2
                            rows = slice(64 * (h % 2), 64 * (h % 2) + 64)
                            qsl = slice(qb * 512, (qb + 1) * 512)
                            P.op("pe", lambda e, b=b: e.matmul(ps_b[0:64, :], ones_f[64:65, 0:64], rd2[b][64:65, :], start=True, stop=True),
                                 reads=["ones_f", f"rdx{b}"], writes=["psb"])
                            P.op("dve", lambda e: e.tensor_copy(bcs[:], ps_b[0:64, :]), reads=["psb"], writes=["bcs"])
                            P.op("pool", lambda e, b=b, par=par: e.tensor_tensor(bg[:], bcs[:], glc2[par][b][:], ALU.mult),
                                 reads=["bcs", f"glcx{par}{b}"], writes=["bg"])
                            P.op("dve", lambda e, b=b: e.tensor_tensor(yoc[b][:], osb[b][0:64, :], bg[:], ALU.mult),
                                 reads=[f"osb{b}", "bg"], writes=[f"yoc{b}"])
                            P.dma("sp", lambda e, b=b, cch=cch, rows=rows, qsl=qsl: e.dma_start(
                                out=YGd[cch][rows, qsl], in_=yoc[b][:]), f"st_yoc{b}", reads=[f"yoc{b}"])

                        load_qt(0)
                        pending = None
                        npair = 0
                        for qb in range(NB):
                            qsl = slice(qb * 512, (qb + 1) * 512)
                            Qs = qb % 2
                            if qb + 1 < NB:
                                load_qt(qb + 1)
                            for g in range(4):
                                par = npair % 2
                                npair += 1
                                for b in range(2):
                                    h = b * 4 + g
                                    cch = 2 + h // 2
                                    rows = slice(64 * (h % 2), 64 * (h % 2) + 64)
                                    P.dma("sp", lambda e, b=b, par=par, cch=cch, rows=rows, qsl=qsl: e.dma_start(
                                        out=glc2[par][b][:], in_=Gd[cch][rows, qsl]), f"ld_glc{par}{b}", writes=[f"glcx{par}{b}"])

                                def qk(kc, g=g, Qs=Qs):
                                    s = kc % 2
                                    for b in range(2):
                                        pr = slice(64 * b, 64 * b + 64)
                                        P.op("pe", lambda e, s=s, b=b, kc=kc, pr=pr: e.matmul(
                                            ps_s[s][:, b, :], KT[pr, kc * 128:(kc + 1) * 128], qtb[Qs][pr, g, :], start=True, stop=True),
                                            reads=["KT", f"qtb{Qs}"], writes=[f"pss{s}"])

                                qk(0)
                                qk(1)
                                for kc in range(KC):
                                    s = kc % 2
                                    ts = pcnt % 3
                                    pcnt += 1
                                    P.op("act", lambda e, s=s, ts=ts: e.activation(pT[ts][:], ps_s[s][:], AF.Exp, bias=negM[:], scale=scale),
                                         reads=[f"pss{s}", "negM"], writes=[f"pT{ts}"])
                                    if kc + 2 < KC:
                                        qk(kc + 2)
                                    for b in range(2):
                                        P.op("pe", lambda e, ts=ts, b=b, kc=kc: e.matmul(
                                            ps_o[b][0:65, :], V[:, kc, b, 0:65], pT[ts][:, b, :], start=(kc == 0), stop=(kc == KC - 1)),
                                            reads=["V", f"pT{ts}"], writes=[f"pso{b}"])
                                    if pending is not None and kc == min(8, KC - 1):
                                        epi2(*pending, 0, 1 - par)
                                    if pending is not None and kc == min(16, KC - 1):
                                        epi2(*pending, 1, 1 - par)
                                        pending = None
                                epi1()
                                pending = (qb, g)
                        epi2(*pending, 0, (npair - 1) % 2)
                        epi2(*pending, 1, (npair - 1) % 2)
                b_ops = capture(emit_B)
                c_ops = capture(emit_C)
                P.ops += interleave(c_ops, b_ops)
            P.barrier()

            with ExitStack() as ph, P.phase('D'):
                NY, NX, NZ = 3, 4, 3
                ygt = [sb(f"ygt{i}_{l}", [128, 8, 512], BF16, ph) for i in range(NY)]
                xr = [sb(f"xr{i}_{l}", [128, D], F32, ph) for i in range(NX)]
                zt = [sb(f"zt{i}_{l}", [128, D], F32, ph) for i in range(NZ)]
                junk2 = sb(f"junk2_{l}", [128, D], BF16, ph)
                ss2z = sb(f"ss2z_{l}", [128, NT], F32, ph)
                rs2 = sb(f"rs2_{l}", [128, NT], F32, ph)
                P.op("pool", lambda e: e.memset(ss2z[:], 0.0), writes=["ss2z"])
                ps_z = [ps(f"psz{i}_{l}", [128, 2, 512], F32, ph) for i in range(2)]

                def load_yg(blk):
                    ys = blk % NY
                    P.dma("sp", lambda e, ys=ys, blk=blk: e.dma_start(
                        out=ygt[ys][:], in_=YGd[:, :, blk * 512:(blk + 1) * 512].rearrange("c p t -> p c t")),
                        f"ld_yg{ys}", writes=[f"ygt{ys}"])

                def load_xr(tt):
                    xs_ = tt % NX
                    P.dma("sp", lambda e, xs_=xs_, tt=tt: e.dma_start(out=xr[xs_][:], in_=x_src[tt * 128:(tt + 1) * 128, :]),
                          f"ld_xr{xs_}", writes=[f"xr{xs_}"])

                for b0 in range(min(2, NB)):
                    load_yg(b0)
                for t0 in range(min(3, NT)):
                    load_xr(t0)
                for tt in range(NT):
                    blk, j = divmod(tt, 4)
                    s = tt % 2
                    ys = blk % NY
                    xs_ = tt % NX
                    zs = tt % NZ
                    if j == 0 and blk + 2 < NB:
                        load_yg(blk + 2)
                    if tt + 3 < NT:
                        load_xr(tt + 3)
                    for n in range(2):
                        for k in range(8):
                            P.op("pe", lambda e, s=s, ys=ys, j=j, n=n, k=k: e.matmul(
                                ps_z[s][:, n, :], ygt[ys][:, k, j * 128:(j + 1) * 128], Wob[:, k, n * 512:(n + 1) * 512],
                                start=(k == 0), stop=(k == 7)),
                                reads=[f"ygt{ys}", "Wob"], writes=[f"psz{s}"])
                    zf = ps_z[s][:].rearrange("p a b -> p (a b)")
                    P.op("act", lambda e, s=s, zf=zf, tt=tt: e.activation(junk2[:], zf, AF.Square, accum_out=ss2z[:, tt:tt + 1]),
                         reads=[f"psz{s}", "ss2z"], writes=["junk2", f"ssq2_{tt}"])
                    P.op("dve", lambda e, tt=tt: e.tensor_scalar(rs2[:, tt:tt + 1], ss2z[:, tt:tt + 1], 1.0 / D, EPS, ALU.mult, ALU.add),
                         reads=[f"ssq2_{tt}"], writes=[f"rs2_{tt}"])
                    P.op("pool", lambda e, tt=tt: e.tensor_tensor(rs2[:, tt:tt + 1], rs2[:, tt:tt + 1], mhalf[:, 0:1], ALU.pow),
                         reads=[f"rs2_{tt}"], writes=[f"rs2_{tt}"])
                    P.op("dve", lambda e, s=s, zs=zs, zf=zf, tt=tt: e.scalar_tensor_tensor(zt[zs][:], zf, rs2[:, tt:tt + 1], G2b[:], ALU.mult, ALU.mult),
                         reads=[f"psz{s}", f"rs2_{tt}", "G2b"], writes=[f"zt{zs}"])
                    P.op("dve", lambda e, zs=zs, xs_=xs_: e.tensor_tensor(zt[zs][:], zt[zs][:], xr[xs_][:], ALU.add),
                         reads=[f"zt{zs}", f"xr{xs_}"], writes=[f"zt{zs}"])
                    P.dma("sp", lambda e, zs=zs, tt=tt: e.dma_start(out=x_dst[tt * 128:(tt + 1) * 128, :], in_=zt[zs][:]),
                          f"st_z{zs}", reads=[f"zt{zs}"])
            P.barrier()

        for l_ in range(L):
            emit_layer(l_)

        P.ops = [o for o in P.ops if o["bar"] or o["tag"] in ("S" + phases)]
        global _LAST_PROG
        _LAST_PROG = P
        P.analyze()
        sems = {}
        for k in P.final_counts:
            sems[k] = es.enter_context(nc.semaphore("s_" + "_".join(k)))
        P.emit(sems)
        for k, v in P.final_counts.items():
            if k[0] == "dma":
                nc.sync.wait_ge(sems[k], v)
    return nc


def rope_table(T):
    rows_n = T // 64
    row = np.repeat(np.arange(rows_n, dtype=np.float32), 64)
    col = np.tile(np.arange(64, dtype=np.float32), rows_n)
    n_f = HD // 4
    inv = (np.float32(10000.0) ** (-np.arange(n_f, dtype=np.float32) / np.float32(n_f))).astype(np.float32)
    ang = np.concatenate([row[:, None] * inv, col[:, None] * inv], axis=-1).astype(np.float32)
    return np.concatenate([np.cos(ang), np.sin(ang)], axis=-1).astype(np.float32)


def pool_invcnt(T):
    tab = np.zeros((128, 2, 16), np.float32)
    for c in range(2):
        for eh in range(2):
            w = WINS[2 * c + eh]
            for i in range(8):
                for t, col in ((i, i), (T - 8 + i, 8 + i)):
                    lo = max(t - w // 2, 0)
                    hi = min(t + w // 2, T)
                    tab[64 * eh:64 * eh + 64, c, col] = 1.0 / float(hi - lo)
    return tab


def shared_inputs(T, L, ada_w, ada_b, norm_pre, norm_post, w_in, w_out, pool_w, pool_scale, q_norm, k_norm,
                  conv_w, conv_b, rg_wa, rg_ba, rg_wx, rg_bx, rg_lambda):
    f = lambda a: np.ascontiguousarray(np.asarray(a, dtype=np.float32))
    pvec = np.zeros((128, L, 24), np.float32)
    for l in range(L):
        pvec[:, l, 0:2] = f(pool_scale)[l].reshape(2, 128).T
        cw = f(conv_w)[l]
        for c in range(2):
            pvec[:, l, 2 + c * 4:6 + c * 4] = cw[:, c * 128:(c + 1) * 128].T
        pvec[:, l, 10:12] = f(conv_b)[l].reshape(2, 128).T
        for z in range(2):
            pvec[:, l, 12 + z * 2:14 + z * 2] = f(rg_ba)[l, z].reshape(2, 128).T
            pvec[:, l, 16 + z * 2:18 + z * 2] = f(rg_bx)[l, z].reshape(2, 128).T
            pvec[:, l, 20 + z * 2:22 + z * 2] = f(rg_lambda)[l, z].reshape(2, 128).T
    return {
        "ada_w": f(ada_w)[:L],
        "rowv": np.ascontiguousarray(np.concatenate([f(ada_b)[:L], f(norm_pre)[:L], f(norm_post)[:L]], axis=1)),
        "w_in": f(w_in)[:L],
        "w_out": f(w_out)[:L],
        "pool_w": f(pool_w)[:L],
        "rg_w": np.ascontiguousarray(np.stack([f(rg_wa)[:L], f(rg_wx)[:L]], axis=1)),
        "pvec": pvec,
        "qkn": np.ascontiguousarray(np.concatenate([f(q_norm)[:L], f(k_norm)[:L]], axis=1)),
        "cs_tab": rope_table(T),
        "ident": np.eye(128, dtype=np.float32),
        "invc": pool_invcnt(T),
    }


_NC_CACHE = {}


def kernel(x_prompt, x_sample, c_prompt, c_sample, ada_w, ada_b, norm_pre, norm_post, w_in, w_out,
           pool_w, pool_scale, q_norm, k_norm, conv_w, conv_b, rg_wa, rg_ba, rg_wx, rg_bx, rg_lambda):
    x_prompt = np.asarray(x_prompt, np.float32)
    x_sample = np.asarray(x_sample, np.float32)
    c_prompt = np.asarray(c_prompt, np.float32)
    c_sample = np.asarray(c_sample, np.float32)
    L = int(np.asarray(ada_w).shape[0])
    T = int(x_prompt.shape[1])
    assert x_sample.shape[1] == T
    seqs = [(x_prompt[b], c_prompt[b]) for b in range(x_prompt.shape[0])] + \
           [(x_sample[b], c_sample[b]) for b in range(x_sample.shape[0])]
    nseq = len(seqs)
    assert nseq <= NCORES
    shared = shared_inputs(T, L, ada_w, ada_b, norm_pre, norm_post, w_in, w_out, pool_w, pool_scale, q_norm,
                           k_norm, conv_w, conv_b, rg_wa, rg_ba, rg_wx, rg_bx, rg_lambda)
    in_maps = []
    for i in range(NCORES):
        xs_, cs_ = seqs[i % nseq]
        m = dict(shared)
        m["x"] = np.ascontiguousarray(xs_)
        m["c"] = np.ascontiguousarray(cs_.reshape(8, 128).T)
        in_maps.append(m)
    key = (T, L)
    if key not in _NC_CACHE:
        _NC_CACHE[key] = build_program(T, L)
    res = run_bass_kernel_spmd(_NC_CACHE[key], in_maps, core_ids=list(range(NCORES)))
    outs = [np.asarray(res.results[i]["y"], np.float32) for i in range(nseq)]
    nb = x_prompt.shape[0]
    y_prompt = np.stack(outs[:nb], axis=0)
    y_sample = np.stack(outs[nb:], axis=0)
    return (y_prompt, y_sample)
```
